# Optimizing a Trainium2 kernel written in Bass

```python
import jax, jax.numpy as jnp
from jax import lax
import numpy as np

D_MODEL = 1024
BATCH = 4
SEQ = 4096
DEPTH = 4
DEC_BATCH = 32
DEC_SEQ = 8
PAST_LEN = 8192
PAGE_SIZE = 128

HEAD_DIM = 64
HEADS_PER_GROUP = 4
ATTN_GROUPS = ((128, 1), (512, 4), (2048, 16))
N_HEADS = HEADS_PER_GROUP * len(ATTN_GROUPS)
ATTN_WIDTH = N_HEADS * HEAD_DIM
MERGED_WIDTH = HEADS_PER_GROUP * HEAD_DIM
ROT_DIM = HEAD_DIM // 4
ROPE_THETA = 500000.0
POOL_WINDOWS = (2, 4, 8, 16)
POOL_GROUP_DIM = D_MODEL // 8
POOL_WIDTH = len(POOL_WINDOWS) * POOL_GROUP_DIM
POOL_PAD = max(POOL_WINDOWS) - 1
D_FF = 4 * D_MODEL
IN_WIDTH = 3 * ATTN_WIDTH + POOL_WIDTH + 2 * D_MODEL
Q_BLOCK = 128
RMS_EPS = 1e-6

kernel_name = "hybrid_dilated_pool_decoder_step"


def rmsnorm(x, g):
    xf = x.astype(jnp.float32)
    y = xf * lax.rsqrt(jnp.mean(xf * xf, axis=-1, keepdims=True) + RMS_EPS) * g.astype(jnp.float32)
    return y.astype(x.dtype)


def rope(x, pos):
    half = ROT_DIM // 2
    inv = ROPE_THETA ** (-jnp.arange(0, ROT_DIM, 2, dtype=jnp.float32) / ROT_DIM)
    ang = pos.astype(jnp.float32)[:, None] * inv[None, :]
    cos = jnp.cos(ang)[None, :, None, :]
    sin = jnp.sin(ang)[None, :, None, :]
    xr = x[..., :ROT_DIM].astype(jnp.float32)
    x1, x2 = xr[..., :half], xr[..., half:]
    rot = jnp.concatenate([x1 * cos - x2 * sin, x2 * cos + x1 * sin], axis=-1)
    return jnp.concatenate([rot.astype(x.dtype), x[..., ROT_DIM:]], axis=-1)


def dilated_group(q, k_all, v_all, q_idx, window, dilation):
    n_keys = window // dilation + 1
    idx = q_idx[:, None] - jnp.arange(n_keys)[None, :] * dilation
    valid = idx >= 0
    idx_c = jnp.maximum(idx, 0)
    kg = k_all[:, idx_c]
    vg = v_all[:, idx_c]
    s = jnp.einsum('bthd,btjhd->bhtj', q, kg).astype(jnp.float32) * (HEAD_DIM ** -0.5)
    s = jnp.where(valid[None, None], s, -jnp.inf)
    m = jnp.max(s, axis=-1, keepdims=True)
    p = jnp.exp(s - m)
    den = jnp.sum(p, axis=-1, keepdims=True)
    o = jnp.einsum('bhtj,btjhd->bthd', p / den, vg.astype(jnp.float32))
    lse = jnp.transpose((m + jnp.log(den))[..., 0], (0, 2, 1))
    return o, lse


def dilated_mixture(q, ks, vs, q_idxs):
    outs, lses = [], []
    for g, (w, d) in enumerate(ATTN_GROUPS):
        qg = q[:, :, g * HEADS_PER_GROUP:(g + 1) * HEADS_PER_GROUP]
        o, lse = dilated_group(qg, ks[g], vs[g], q_idxs[g], w, d)
        outs.append(o)
        lses.append(lse)
    wts = jax.nn.softmax(jnp.stack(lses, axis=0), axis=0)
    return jnp.sum(wts[..., None] * jnp.stack(outs, axis=0), axis=0)


def attention_branch(q, ks, vs, offsets):
    B, T = q.shape[0], q.shape[1]
    if T > Q_BLOCK and T % Q_BLOCK == 0:
        nb = T // Q_BLOCK
        qb = jnp.transpose(q.reshape(B, nb, Q_BLOCK, N_HEADS, HEAD_DIM), (1, 0, 2, 3, 4))
        ib = jnp.arange(T, dtype=jnp.int32).reshape(nb, Q_BLOCK)
        ob = lax.map(lambda a: dilated_mixture(a[0], ks, vs, [off + a[1] for off in offsets]), (qb, ib))
        o = jnp.transpose(ob, (1, 0, 2, 3, 4)).reshape(B, T, HEADS_PER_GROUP, HEAD_DIM)
    else:
        t = jnp.arange(T, dtype=jnp.int32)
        o = dilated_mixture(q, ks, vs, [off + t for off in offsets])
    return o.reshape(B, T, MERGED_WIDTH)


def pool_branch(u_ext, pos0, lin, scale):
    B = u_ext.shape[0]
    T = u_ext.shape[1] - POOL_PAD
    uf = u_ext.astype(jnp.float32)
    c = jnp.concatenate([jnp.zeros((B, 1, POOL_WIDTH), jnp.float32), jnp.cumsum(uf, axis=1)], axis=1)
    pos = pos0 + jnp.arange(T, dtype=jnp.int32)
    u_new = uf[:, POOL_PAD:]
    outs = []
    for g, w in enumerate(POOL_WINDOWS):
        sl = slice(g * POOL_GROUP_DIM, (g + 1) * POOL_GROUP_DIM)
        s = c[:, POOL_PAD + 1:POOL_PAD + 1 + T, sl] - c[:, POOL_PAD + 1 - w:POOL_PAD + 1 - w + T, sl]
        cnt = jnp.minimum(w, pos + 1).astype(jnp.float32)[None, :, None]
        outs.append(s / cnt - u_new[..., sl])
    p = jnp.stack(outs, axis=2)
    y = jnp.einsum('btgc,gcd->btgd', p, lin.astype(jnp.float32)).reshape(B, T, POOL_WIDTH)
    return (y * scale.astype(jnp.float32)).astype(u_ext.dtype)


def layer(x, pos, pos0, kv_bufs, pool_buf, norm1, w_in, w_pa, w_pb, pool_lin, pool_scale, w_o, norm2, w_up, w_down):
    B, T, _ = x.shape
    h = rmsnorm(x, norm1)
    proj = h @ w_in
    c1, c2, c3 = ATTN_WIDTH, 2 * ATTN_WIDTH, 3 * ATTN_WIDTH
    c4 = c3 + POOL_WIDTH
    c5 = c4 + D_MODEL
    q, k, v, u, ga, gb = jnp.split(proj, [c1, c2, c3, c4, c5], axis=-1)
    q = rope(q.reshape(B, T, N_HEADS, HEAD_DIM), pos)
    k = rope(k.reshape(B, T, N_HEADS, HEAD_DIM), pos)
    v = v.reshape(B, T, N_HEADS, HEAD_DIM)
    ks, vs, offs, new_kv = [], [], [], []
    for g, (w, d) in enumerate(ATTN_GROUPS):
        kg = k[:, :, g * HEADS_PER_GROUP:(g + 1) * HEADS_PER_GROUP]
        vg = v[:, :, g * HEADS_PER_GROUP:(g + 1) * HEADS_PER_GROUP]
        if kv_bufs is None:
            k_all, v_all, off, keep = kg, vg, 0, min(w, T)
        else:
            buf = kv_bufs[g]
            L = buf.shape[1]
            k_all = jnp.concatenate([buf[:, :, 0], kg], axis=1)
            v_all = jnp.concatenate([buf[:, :, 1], vg], axis=1)
            off, keep = L, L
        ks.append(k_all)
        vs.append(v_all)
        offs.append(off)
        new_kv.append(jnp.stack([k_all[:, -keep:], v_all[:, -keep:]], axis=2))
    o_a = attention_branch(q, ks, vs, offs).astype(x.dtype)
    if pool_buf is None:
        pool_buf = jnp.zeros((B, POOL_PAD, POOL_WIDTH), u.dtype)
    u_ext = jnp.concatenate([pool_buf, u], axis=1)
    new_pool = u_ext[:, -POOL_PAD:]
    o_b = pool_branch(u_ext, pos0, pool_lin, pool_scale)
    mixed = jax.nn.sigmoid(ga) * (o_a @ w_pa) + jax.nn.sigmoid(gb) * (o_b @ w_pb)
    x = x + mixed @ w_o
    h2 = rmsnorm(x, norm2)
    x = x + jnp.square(jax.nn.relu(h2 @ w_up)) @ w_down
    return x, new_kv, new_pool


def setup_inputs(seed: int = 0) -> dict:
    key = jax.random.key(seed)
    ks = jax.random.split(key, 20)
    f32 = jnp.float32
    nrm = lambda k, shape, s: jax.random.normal(k, shape, f32) * s
    win = [min(w, PAST_LEN) for (w, _) in ATTN_GROUPS]
    return {
        "x_prompt": nrm(ks[0], (BATCH, SEQ, D_MODEL), 1.0),
        "x_sample": nrm(ks[1], (DEC_BATCH, DEC_SEQ, D_MODEL), 1.0),
        "cache_kv_w128": nrm(ks[2], (DEPTH, DEC_BATCH, win[0], 2, HEADS_PER_GROUP, HEAD_DIM), 1.0),
        "cache_kv_w512": nrm(ks[3], (DEPTH, DEC_BATCH, win[1], 2, HEADS_PER_GROUP, HEAD_DIM), 1.0),
        "cache_kv_w2048": nrm(ks[4], (DEPTH, DEC_BATCH, win[2], 2, HEADS_PER_GROUP, HEAD_DIM), 1.0),
        "state_pool": nrm(ks[5], (DEPTH, DEC_BATCH, POOL_PAD, POOL_WIDTH), 1.0),
        "norm1": 1.0 + nrm(ks[6], (DEPTH, D_MODEL), 0.05),
        "w_in": nrm(ks[7], (DEPTH, D_MODEL, IN_WIDTH), D_MODEL ** -0.5),
        "w_pa": nrm(ks[8], (DEPTH, MERGED_WIDTH, D_MODEL), MERGED_WIDTH ** -0.5),
        "w_pb": nrm(ks[9], (DEPTH, POOL_WIDTH, D_MODEL), POOL_WIDTH ** -0.5),
        "pool_lin": nrm(ks[10], (DEPTH, len(POOL_WINDOWS), POOL_GROUP_DIM, POOL_GROUP_DIM), POOL_GROUP_DIM ** -0.5),
        "pool_scale": 1.0 + nrm(ks[11], (DEPTH, POOL_WIDTH), 0.05),
        "w_o": nrm(ks[12], (DEPTH, D_MODEL, D_MODEL), D_MODEL ** -0.5),
        "norm2": 1.0 + nrm(ks[13], (DEPTH, D_MODEL), 0.05),
        "w_up": nrm(ks[14], (DEPTH, D_MODEL, D_FF), D_MODEL ** -0.5),
        "w_down": nrm(ks[15], (DEPTH, D_FF, D_MODEL), D_FF ** -0.5),
        "final_norm": 1.0 + nrm(ks[16], (D_MODEL,), 0.05),
    }


def reference(x_prompt, x_sample, cache_kv_w128, cache_kv_w512, cache_kv_w2048, state_pool,
              norm1, w_in, w_pa, w_pb, pool_lin, pool_scale, w_o, norm2, w_up, w_down, final_norm):
    pos_p = jnp.arange(SEQ, dtype=jnp.int32)
    pos_s = PAST_LEN + jnp.arange(DEC_SEQ, dtype=jnp.int32)
    xp, xs = x_prompt, x_sample
    kvp = [[], [], []]
    kvs = [[], [], []]
    poolp, pools = [], []
    for l in range(DEPTH):
        wl = (norm1[l], w_in[l], w_pa[l], w_pb[l], pool_lin[l], pool_scale[l], w_o[l], norm2[l], w_up[l], w_down[l])
        xp, nkv_p, npool_p = layer(xp, pos_p, 0, None, None, *wl)
        bufs = [cache_kv_w128[l], cache_kv_w512[l], cache_kv_w2048[l]]
        xs, nkv_s, npool_s = layer(xs, pos_s, PAST_LEN, bufs, state_pool[l], *wl)
        for g in range(3):
            kvp[g].append(nkv_p[g])
            kvs[g].append(nkv_s[g])
        poolp.append(npool_p)
        pools.append(npool_s)
    y_prompt = rmsnorm(xp, final_norm)
    y_sample = rmsnorm(xs, final_norm)
    kv_w128_prompt = jnp.stack(kvp[0], axis=0)
    kv_w512_prompt = jnp.stack(kvp[1], axis=0)
    kv_w2048_prompt = jnp.stack(kvp[2], axis=0)
    pool_prompt = jnp.stack(poolp, axis=0)
    kv_w128_sample = jnp.stack(kvs[0], axis=0)
    kv_w512_sample = jnp.stack(kvs[1], axis=0)
    kv_w2048_sample = jnp.stack(kvs[2], axis=0)
    pool_sample = jnp.stack(pools, axis=0)
    return (y_prompt, y_sample, kv_w128_prompt, kv_w512_prompt, kv_w2048_prompt, pool_prompt,
            kv_w128_sample, kv_w512_sample, kv_w2048_sample, pool_sample)
```

```python
from contextlib import ExitStack
import numpy as np
import concourse.bass as bass
import concourse.mybir as mybir
from concourse.bass_utils import run_bass_kernel_spmd

F32 = mybir.dt.float32
BF16 = mybir.dt.bfloat16
ALU = mybir.AluOpType
AF = mybir.ActivationFunctionType

D = 1024
FULLSEQ = 4096
SEQ = 2048
NSMP = 32
NTOK = SEQ + NSMP
NBLK = 17
NPB = 16
PAIRS = [[0, 1], [2, 3], [4, 5], [6, 7]]
DEPTH = 4
INW = 4864
GROUPS = ((128, 1), (512, 4), (2048, 16))
POOLW = (2, 4, 8, 16)
SEM_LIMIT = 30000
NSLOT = 32
SAME_ENGINE_WAIT = True
ENGS = ("sync", "tensor", "vector", "scalar", "gpsimd")


def _is_ap(x):
    return hasattr(x, "ap") and hasattr(x, "offset") and hasattr(x, "space")


def _is_dram(ap):
    return "DRAM" in str(ap.space).upper()


def _region(ap):
    isz = 4 if ap.dtype == F32 else 2
    dims = ap.ap
    off = ap.offset
    if _is_dram(ap):
        ext = sum(abs(s) * (c - 1) for s, c in dims)
        return (ap.name, 0, 1, off * isz, (off + ext + 1) * isz)
    if "PSUM" in str(ap.space).upper():
        return (ap.name, 0, 128, 0, 1 << 30)
    pstep, pcount = dims[0]
    p0 = off // pstep if pstep > 0 else 0
    f0 = off - p0 * pstep
    ext = sum(abs(s) * (c - 1) for s, c in dims[1:])
    return (ap.name, p0, p0 + pcount, f0 * isz, (f0 + ext + 1) * isz)


def _ovl(a, b):
    return a[1] < b[2] and b[1] < a[2] and a[3] < b[4] and b[3] < a[4]


def _contains(a, b):
    return a[1] <= b[1] and b[2] <= a[2] and a[3] <= b[3] and b[4] <= a[4]


class Em:
    def __init__(self, sems):
        self.sems = sems
        self.next_sem = 0
        self.cur = {}
        self.streams = {k: [] for k in ENGS}
        self.recs = {}
        self.known = {k: {} for k in ENGS}
        self.slots = {"sync": [None] * NSLOT, "scalar": [None] * 8}
        self.slot_n = {"sync": 0, "scalar": 0}
        self.pending = []
        self.seen_load = False
        self.nops = 0

    def _new_sem(self):
        i = self.next_sem
        self.next_sem += 1
        assert self.next_sem <= len(self.sems), "out of semaphores"
        return i

    def _alloc(self, eng):
        if eng not in self.cur or self.cur[eng][1] + 1 > SEM_LIMIT:
            self.cur[eng] = [self._new_sem(), 0]
        st = self.cur[eng]
        st[1] += 1
        return st[0], st[1]

    def _deps(self, eng, reads, writes, ref):
        deps = []
        for r in reads:
            lst = self.recs.setdefault(r[0], [])
            for rec in lst:
                if rec[1] and _ovl(rec[0], r):
                    deps.append(rec[2])
            lst[:] = [rec for rec in lst if not ((not rec[1]) and rec[2][3] == eng and rec[2][0] == ref[0]
                                                 and _contains(r, rec[0]))]
            lst.append((r, False, ref))
        for w in writes:
            lst = self.recs.setdefault(w[0], [])
            keep = []
            for rec in lst:
                if _ovl(rec[0], w):
                    if rec[2] is not ref:
                        deps.append(rec[2])
                    if _contains(w, rec[0]):
                        continue
                keep.append(rec)
            keep.append((w, True, ref))
            self.recs[w[0]] = keep
        return deps

    def _waits(self, eng, deps, is_dma):
        waits = []
        kn = self.known[eng]
        for kind, si, val, deng in deps:
            if kind == "c" and deng == eng and not is_dma and (eng == "tensor" or not SAME_ENGINE_WAIT):
                continue
            if kn.get(si, 0) >= val:
                continue
            kn[si] = val
            waits.append((si, val))
        return waits

    def flush(self):
        for deps, out, in_, si, ref in self.pending:
            waits = self._waits("sync", deps, True)
            self.streams["sync"].append((waits, "dma_start", (), {"out": out, "in_": in_}, si, 16))
        self.pending = []
        self.seen_load = False

    def op(self, eng, meth, a, k):
        if self.pending and self.seen_load:
            self.flush()
        outs = []
        if "out" in k:
            outs.append(k["out"])
        else:
            outs.append(a[0])
        if k.get("accum_out", None) is not None:
            outs.append(k["accum_out"])
        ins = [x for x in list(a[(0 if "out" in k else 1):]) if _is_ap(x)]
        ins += [v for kk, v in k.items() if kk not in ("out", "accum_out") and _is_ap(v)]
        si, val = self._alloc(eng)
        ref = ("c", si, val, eng)
        psum_ins = [x for x in ins if "PSUM" in str(x.space).upper()]
        ins = [x for x in ins if "PSUM" not in str(x.space).upper()]
        deps = self._deps(eng, [_region(x) for x in ins], [_region(x) for x in outs + psum_ins], ref)
        if self.pending:
            prefs = [p[4] for p in self.pending]
            if any(any(d is p for p in prefs) for d in deps):
                self.flush()
        waits = self._waits(eng, deps, False)
        self.streams[eng].append((waits, meth, a, k, si, 1))
        self.nops += 1

    def dma(self, out, in_, queue="sync"):
        is_store = _is_dram(out) and not _is_dram(in_) and queue == "sync"
        nsl = len(self.slots[queue])
        sl = self.slot_n[queue] % nsl
        self.slot_n[queue] += 1
        prev = self.slots[queue][sl]
        if prev is None:
            si, val = self._new_sem(), 16
        else:
            si, val = prev[1], prev[2] + 16
        ref = ("d", si, val, queue)
        deps = self._deps(queue, [_region(in_)], [_region(out)], ref)
        if prev is not None:
            deps.append(prev)
        self.slots[queue][sl] = ref
        self.nops += 1
        if queue != "sync":
            if self.pending:
                prefs = [p[4] for p in self.pending]
                if any(any(d is p for p in prefs) for d in deps):
                    self.flush()
            waits = self._waits(queue, deps, True)
            self.streams[queue].append((waits, "dma_start", (), {"out": out, "in_": in_}, si, 16))
            return
        if is_store:
            self.pending.append((deps, out, in_, si, ref))
            return
        if self.pending:
            prefs = [p[4] for p in self.pending]
            if any(any(d is p for p in prefs) for d in deps):
                self.flush()
        waits = self._waits("sync", deps, True)
        self.streams["sync"].append((waits, "dma_start", (), {"out": out, "in_": in_}, si, 16))
        self.seen_load = True

    def collective(self, in_ap, out_ap):
        if self.pending:
            self.flush()
        si = self._new_sem()
        ref = ("d", si, 1, "gpsimd")
        deps = self._deps("gpsimd", [_region(in_ap)], [_region(out_ap)], ref)
        waits = self._waits("gpsimd", deps, True)
        self.streams["gpsimd"].append((waits, "collective_compute", ("AllGather", ALU.bypass),
                                       dict(replica_groups=PAIRS, ins=[in_ap.opt()], outs=[out_ap.opt()]), si, None))
        self.nops += 1

    def replay(self, eng, handle):
        for waits, meth, a, k, si, inc in self.streams[eng]:
            for wsi, wval in waits:
                handle.wait_ge(self.sems[wsi], wval)
            inst = getattr(handle, meth)(*a, **k)
            if inc is None:
                inst.then_inc(self.sems[si])
            else:
                inst.then_inc(self.sems[si], inc)
        if eng == "sync":
            for q in self.slots:
                for ref in self.slots[q]:
                    if ref is not None:
                        handle.wait_ge(self.sems[ref[1]], ref[2])


def build_program():
    nc = bass.Bass("TRN2", target_bir_lowering=False)

    def din(name, shape, dt=F32):
        return nc.dram_tensor(name, list(shape), dt, kind="ExternalInput").ap()

    def dout(name, shape):
        return nc.dram_tensor(name, list(shape), F32, kind="ExternalOutput").ap()

    def dscr(name, shape, dt=F32):
        return nc.dram_tensor(name, list(shape), dt, kind="Internal").ap()

    xin = din("xin", [NTOK, D])
    caches = [din("cache%d" % g, [DEPTH, 4, GROUPS[g][0], 512]) for g in range(3)]
    spool = din("spool", [DEPTH, 4, 15, 512])
    norm1 = din("norm1", [DEPTH, D])
    norm2 = din("norm2", [DEPTH, D])
    fnorm = din("fnorm", [1, D])
    w_in = din("w_in", [DEPTH, D, INW])
    w_pa = din("w_pa", [DEPTH, 256, D])
    w_pb = din("w_pb", [DEPTH, 512, D])
    plin = din("plin", [DEPTH, 4, 128, 128])
    pscaleT = din("pscaleT", [DEPTH, 128, 4])
    w_o = din("w_o", [DEPTH, D, D])
    w_up = din("w_up", [DEPTH, D, 4 * D])
    w_down = din("w_down", [DEPTH, 4 * D, D])
    cstab = din("cstab", [NTOK, 16])
    ident_in = din("ident", [128, 128])
    mask_in = din("maskpc", [128, 256])
    smask_in = din("smask", [3, 2, 128, 128])
    snew_in = din("snew", [8, 96])
    rcfix_in = din("rcfix", [4, 128, 16])
    flag_in = din("flag", [128, 1])

    y = dout("y", [NTOK, D])
    kvp = [dout("kvp%d" % g, [DEPTH, GROUPS[g][0], 512]) for g in range(3)]
    poolp = dout("poolp", [DEPTH, 15, 512])
    kvs = [dout("kvs%d" % g, [DEPTH, 4, GROUPS[g][0], 512]) for g in range(3)]
    pools = dout("pools", [DEPTH, 4, 15, 512])

    X = dscr("X", [NTOK, D])
    X2 = dscr("X2", [NTOK, D])
    P = dscr("P", [NTOK, INW])
    UT = dscr("UT", [4, 128, NTOK])
    OBT = dscr("OBT", [4, 128, NTOK], BF16)
    OG = dscr("OG", [3, SEQ, 260])
    OA = dscr("OA", [NSMP, 256])
    WBin = [dscr("WBin%d" % p, [128, 8 * INW], BF16) for p in range(2)]
    WBpa = [dscr("WBpa%d" % p, [64, 4 * D], BF16) for p in range(2)]
    WBpb = [dscr("WBpb%d" % p, [128, 4 * D], BF16) for p in range(2)]
    WBo = [dscr("WBo%d" % p, [128, 8 * D], BF16) for p in range(2)]
    WBup = [[dscr("WBup%d_%d" % (p, h), [128, 8 * 2048], BF16) for h in range(2)] for p in range(2)]
    WBdn = [[dscr("WBdn%d_%d" % (p, h), [128, 16 * D], BF16) for h in range(2)] for p in range(2)]
    XB = dscr("XB", [3072, 512])
    XGs = [dscr("XG%d" % j, [2048, 512]) for j in range(3)]

    with ExitStack() as es:
        sems = [es.enter_context(nc.semaphore("s%d" % i)) for i in range(90)]
        em = Em(sems)

        def sb(name, shape, dt=F32):
            return es.enter_context(nc.sbuf_tensor("sb_" + name, list(shape), dt))

        def ps(name, shape, dt=F32):
            return es.enter_context(nc.psum_tensor("ps_" + name, list(shape), dt))

        psAs = [ps("psA0", [128, 512]), ps("psA1", [128, 512])]
        psB = ps("psB", [128, 512])
        psC = ps("psC", [128, 512])
        psDs = [ps("psD0", [128, 512]), ps("psD1", [128, 512])]
        psTs = [ps("psT0", [128, 1024], BF16), ps("psT1", [128, 1024], BF16)]
        pa_n = [0]
        pt_n = [0]

        def next_psA():
            pa_n[0] += 1
            return psAs[pa_n[0] & 1]

        def next_psT():
            pt_n[0] += 1
            return psTs[pt_n[0] & 1]

        ident_f = sb("ident_f", [128, 128])
        ident_b = sb("ident_b", [128, 128], BF16)
        mask_f = sb("mask_f", [128, 256])
        mask_b = sb("mask_b", [128, 256], BF16)
        smask_f = sb("smask_f", [128, 6, 128])
        smask_b = sb("smask_b", [128, 6, 128], BF16)
        snew_f = sb("snew_f", [8, 96])
        snew_b = sb("snew_b", [8, 96], BF16)
        rcfix = sb("rcfix", [128, 4, 16])

        V = lambda m, *a, **k: em.op("vector", m, a, k)
        A = lambda m, *a, **k: em.op("scalar", m, a, k)
        G = lambda m, *a, **k: em.op("gpsimd", m, a, k)
        T = lambda m, *a, **k: em.op("tensor", m, a, k)

        em.dma(ident_f[:], ident_in[:, :])
        em.dma(mask_f[:], mask_in[:, :])
        em.dma(smask_f[:], smask_in.rearrange("g v p q -> p (g v) q"))
        em.dma(snew_f[:], snew_in[:, :])
        em.dma(rcfix[:], rcfix_in.rearrange("c p t -> p c t"))
        flag = sb("flag", [128, 1])
        em.dma(flag[:], flag_in[:, :])
        V("tensor_copy", out=ident_b[:], in_=ident_f[:])
        V("tensor_copy", out=mask_b[:], in_=mask_f[:])
        V("tensor_copy", out=smask_b[:], in_=smask_f[:])
        V("tensor_copy", out=snew_b[:], in_=snew_f[:])

        WBIG = sb("WBIG", [128, 40960], BF16)
        FA = sb("FA", [128, 9728])
        STGs = [sb("STG%d" % i, [128, 640]) for i in range(3)]
        STGBs = [sb("STGB%d" % i, [128, 640], BF16) for i in range(3)]
        stg_n = [0]
        xts = [sb("xt0", [128, D]), sb("xt1", [128, D])]
        junks = [sb("junk0", [128, D]), sb("junk1", [128, D])]
        gbc = sb("gbc", [128, D])
        hbs = [sb("hb0", [128, D], BF16), sb("hb1", [128, D], BF16)]
        hTs = [sb("hT0", [128, 8, 128], BF16), sb("hT1", [128, 8, 128], BF16)]
        sss = [sb("ss0", [128, 4]), sb("ss1", [128, 4])]
        projs = [FA[:, 0:INW], FA[:, INW:2 * INW]]
        proj = projs[0]
        css = [sb("cs0", [128, 16]), sb("cs1", [128, 16])]
        utbs = [sb("utb0", [128, 4, 128]), sb("utb1", [128, 4, 128])]
        rt0 = sb("rt0", [128, 4, 24, 8])
        rts = [rt0, rt0]
        sm1s = [sb("sm1a", [128, 1024], BF16), sb("sm1b", [128, 1024], BF16)]
        sm2s = [sb("sm2a", [128, 1024], BF16), sb("sm2b", [128, 1024], BF16)]
        KT = [sb("KT%d" % i, [64, 4, 128], BF16) for i in range(3)]
        KTxs = [sb("KTx0", [64, 4, 128], BF16), sb("KTx1", [64, 4, 128], BF16)]
        VExs = [sb("VEx0", [128, 4, 65], BF16), sb("VEx1", [128, 4, 65], BF16)]
        QTs = [sb("QT0", [64, 4, 128], BF16), sb("QT1", [64, 4, 128], BF16)]
        VE = [sb("VE%d" % i, [128, 4, 65], BF16) for i in range(3)]
        QTS = sb("QTS", [64, 12, 32], BF16)
        KTS = sb("KTS", [64, 12, 32], BF16)
        VES = sb("VES", [8, 12, 65], BF16)
        accs = [sb("acc0", [128, 4, 65]), sb("acc1", [128, 4, 65])]
        acc2s = [sb("acc2a", [128, 4, 65]), sb("acc2b", [128, 4, 65])]
        acc3s = [sb("acc3a", [128, 4, 65]), sb("acc3b", [128, 4, 65])]
        rds = [sb("rd0", [128, 4, 1]), sb("rd1", [128, 4, 1])]
        pscale = sb("pscale", [128, 4])
        fix16 = sb("fix16", [128, 16])
        sss_big = [sb("spt0", [16, 512]), sb("spt1", [16, 512])]
        oaTs = [KT[0], KT[1]]
        ogs = [FA[:, 2560:2820], FA[:, 2820:3080]]
        qkvs = [FA[:, 0:768], FA[:, 768:1536]]
        cts = [FA[:, 1536:2048], FA[:, 2048:2560]]

        for i in range(3):
            V("memset", VE[i][:], 1.0)
            V("memset", KT[i][:], 0.0)
        for i in range(2):
            V("memset", VExs[i][:], 0.0)
            V("memset", KTxs[i][:], 0.0)
        V("memset", VES[:], 1.0)
        for i in range(2):
            V("tensor_copy", out=VExs[i][:, :, 64:65], in_=flag[:, 0:1].unsqueeze(1).to_broadcast([128, 4, 1]))

        def load_weight(dst_view, src_ap, rows, cols):
            c0 = 0
            while c0 < cols:
                n = min(640, cols - c0)
                stg_n[0] += 1
                STG = STGs[stg_n[0] % 3]
                em.dma(STG[:rows, 0:n], src_ap[:, c0:c0 + n])
                ce = stg_n[0] % 3
                if ce == 0:
                    G("tensor_copy", out=dst_view[:, c0:c0 + n], in_=STG[:rows, 0:n])
                elif ce == 1:
                    V("tensor_copy", out=dst_view[:, c0:c0 + n], in_=STG[:rows, 0:n])
                else:
                    A("copy", out=dst_view[:, c0:c0 + n], in_=STG[:rows, 0:n])
                c0 += n

        bgq = []

        def pc_add(dst_ap, src_ap, rows, cols):
            c0 = 0
            while c0 < cols:
                n = min(640, cols - c0)
                bgq.append((dst_ap[:, c0:c0 + n], src_ap[:, c0:c0 + n], rows, n))
                c0 += n

        def bg_step(k=1):
            for _ in range(k):
                if not bgq:
                    return
                dst, src, rows, n = bgq.pop(0)
                stg_n[0] += 1
                i3 = stg_n[0] % 3
                STG, STGB = STGs[i3], STGBs[i3]
                em.dma(STG[:rows, 0:n], src)
                if i3 == 0:
                    G("tensor_copy", out=STGB[:rows, 0:n], in_=STG[:rows, 0:n])
                elif i3 == 1:
                    V("tensor_copy", out=STGB[:rows, 0:n], in_=STG[:rows, 0:n])
                else:
                    A("copy", out=STGB[:rows, 0:n], in_=STG[:rows, 0:n])
                em.dma(dst, STGB[:rows, 0:n])

        def bg_flush():
            bg_step(len(bgq))

        def precast_queue(L):
            p = L & 1
            for k in range(8):
                pc_add(WBin[p][:, k * INW:(k + 1) * INW], w_in[L, k * 128:(k + 1) * 128, :], 128, INW)
            for h in range(4):
                pc_add(WBpa[p][:, h * D:(h + 1) * D], w_pa[L, 64 * h:64 * h + 64, :], 64, D)
            for k in range(4):
                pc_add(WBpb[p][:, k * D:(k + 1) * D], w_pb[L, 128 * k:128 * k + 128, :], 128, D)
            for k in range(8):
                pc_add(WBo[p][:, k * D:(k + 1) * D], w_o[L, 128 * k:128 * k + 128, :], 128, D)
            for hf in range(2):
                for k in range(8):
                    pc_add(WBup[p][hf][:, k * 2048:(k + 1) * 2048],
                           w_up[L, 128 * k:128 * k + 128, 2048 * hf:2048 * hf + 2048], 128, 2048)
                for k in range(16):
                    pc_add(WBdn[p][hf][:, k * D:(k + 1) * D],
                           w_down[L, 2048 * hf + 128 * k:2048 * hf + 128 * k + 128, :], 128, D)

        def rmsnorm_rows(xt, junk, ss, rows):
            V("scalar_tensor_tensor", out=junk[:rows], in0=xt[:rows], scalar=1.0, in1=xt[:rows],
              op0=ALU.mult, op1=ALU.mult, accum_out=ss[:rows, 0:1])
            V("tensor_scalar", out=ss[:rows, 1:2], in0=ss[:rows, 0:1], scalar1=1.0 / D, scalar2=1e-6,
              op0=ALU.mult, op1=ALU.add)
            A("activation", out=ss[:rows, 2:3], in_=ss[:rows, 1:2], func=AF.Sqrt)
            V("reciprocal", out=ss[:rows, 3:4], in_=ss[:rows, 2:3])

        def transposes_bf(src, rows, nchunk, width, dst, coff=0):
            psT = next_psT()
            pv = psT[:, 0:nchunk * 128].rearrange("p (c t) -> p c t", c=nchunk)
            for k in range(nchunk):
                T("transpose", pv[:width, k, :rows], src[:rows, k * width:(k + 1) * width], ident_b[:rows, :rows])
            V("tensor_copy", out=dst[:width, coff:coff + nchunk, :rows], in_=pv[:width, :, :rows])

        for l in range(DEPTH):
            xsrc = xin if l == 0 else X
            bg_flush()
            if l + 1 < DEPTH:
                precast_queue(l + 1)
            wp = l & 1
            for g, (W, d) in enumerate(GROUPS):
                for s_ in range(4):
                    em.dma(kvs[g][l, s_, 0:W - 8, :], caches[g][l, s_, 8:W, :], queue="scalar")
            WIN = WBIG[:, 0:8 * INW].rearrange("p (k c) -> p k c", k=8)
            if l == 0:
                for k in range(8):
                    load_weight(WIN[:, k, :], w_in[l, k * 128:(k + 1) * 128, :], 128, INW)
            else:
                for k in range(8):
                    em.dma(WIN[:, k, :], WBin[wp][:, k * INW:(k + 1) * INW])
            em.dma(gbc[:], norm1[l:l + 1, :].partition_broadcast(128))
            def a_bufs(tb):
                pr = tb & 1
                rows = 128 if tb < NPB else NSMP
                return rows, tb * 128, xts[pr], junks[pr], sss[pr], hbs[pr], hTs[pr], css[pr], utbs[pr], rts[pr], projs[pr]

            def a_pre(tb):
                rows, r0, xt, junk, ss, hb, hT, cs, utb, rt, pj = a_bufs(tb)
                em.dma(xt[:rows], xsrc[r0:r0 + rows, :])
                em.dma(cs[:rows], cstab[r0:r0 + rows, :])
                rmsnorm_rows(xt, junk, ss, rows)
                V("scalar_tensor_tensor", out=hb[:rows], in0=xt[:rows], scalar=ss[:rows, 3:4],
                  in1=gbc[:rows], op0=ALU.mult, op1=ALU.mult)

            a_tp = {}

            def a_T(tb):
                rows, r0, xt, junk, ss, hb, hT, cs, utb, rt, pj = a_bufs(tb)
                psT = next_psT()
                pv = psT[:, 0:1024].rearrange("p (c t) -> p c t", c=8)
                for k in range(8):
                    T("transpose", pv[:, k, :rows], hb[:rows, k * 128:(k + 1) * 128], ident_b[:rows, :rows])
                a_tp[tb] = pv

            def a_C(tb):
                rows, r0, xt, junk, ss, hb, hT, cs, utb, rt, pj = a_bufs(tb)
                V("tensor_copy", out=hT[:, 0:8, :rows], in_=a_tp.pop(tb)[:, :, :rows])

            def a_main(tb):
                rows, r0, xt, junk, ss, hb, hT, cs, utb, rt, pj = a_bufs(tb)
                for cg in range(10):
                    c0 = cg * 512
                    n = min(512, INW - c0)
                    psA = next_psA()
                    for k in range(8):
                        T("matmul", psA[:rows, 0:n], lhsT=hT[:, k, :rows], rhs=WIN[:, k, c0:c0 + n],
                          start=(k == 0), stop=(k == 7))
                    segs = []
                    if c0 + n <= 2816:
                        segs.append((c0, c0 + n, False))
                    elif c0 >= 2816:
                        segs.append((c0, c0 + n, True))
                    else:
                        segs.append((c0, 2816, False))
                        segs.append((2816, c0 + n, True))
                    for a, b, sig in segs:
                        if sig:
                            A("activation", out=pj[:rows, a:b], in_=psA[:rows, a - c0:b - c0], func=AF.Sigmoid)
                        else:
                            V("tensor_copy", out=pj[:rows, a:b], in_=psA[:rows, a - c0:b - c0])

            def a_U(tb):
                rows, r0, xt, junk, ss, hb, hT, cs, utb, rt, pj = a_bufs(tb)
                puv = psB[:, :].rearrange("p (c t) -> p c t", c=4)
                for c in range(4):
                    for k in range(8):
                        T("matmul", puv[:, c, :rows], lhsT=WIN[:, k, 2304 + 128 * c:2304 + 128 * (c + 1)],
                          rhs=hT[:, k, :rows], start=(k == 0), stop=(k == 7))

            def a_post(tb):
                rows, r0, xt, junk, ss, hb, hT, cs, utb, rt, pj = a_bufs(tb)
                puv = psB[:, :].rearrange("p (c t) -> p c t", c=4)
                A("copy", out=utb[:, :, :rows], in_=puv[:, :, :rows])
                em.dma(UT.rearrange("c f t -> f c t")[:, :, r0:r0 + rows], utb[:, :, :rows])
                pv = pj[:, 0:1536].rearrange("p (h e) -> p h e", h=24)
                x1 = pv[:rows, :, 0:8]
                x2 = pv[:rows, :, 8:16]
                cosb = cs[:rows, 0:8].unsqueeze(1).to_broadcast([rows, 24, 8])
                sinb = cs[:rows, 8:16].unsqueeze(1).to_broadcast([rows, 24, 8])
                V("tensor_tensor", out=rt[:rows, 0], in0=x1, in1=cosb, op=ALU.mult)
                V("tensor_tensor", out=rt[:rows, 1], in0=x2, in1=sinb, op=ALU.mult)
                V("tensor_tensor", out=rt[:rows, 2], in0=x1, in1=sinb, op=ALU.mult)
                V("tensor_tensor", out=rt[:rows, 3], in0=x2, in1=cosb, op=ALU.mult)
                V("tensor_tensor", out=x1, in0=rt[:rows, 0], in1=rt[:rows, 1], op=ALU.subtract)
                V("tensor_tensor", out=x2, in0=rt[:rows, 3], in1=rt[:rows, 2], op=ALU.add)
                em.dma(P[r0:r0 + rows, :], pj[:rows, :])

            a_pre(0)
            a_T(0)
            a_C(0)
            for tb in range(NBLK):
                if tb + 1 < NBLK:
                    a_pre(tb + 1)
                bg_step(2)
                a_main(tb)
                if tb + 1 < NBLK:
                    a_T(tb + 1)
                a_U(tb)
                if tb + 1 < NBLK:
                    a_C(tb + 1)
                a_post(tb)

            xsec = {2: 0, 1: 2048, 0: 2560}
            for g, (W, d) in enumerate(GROUPS):
                em.dma(XB[xsec[g]:xsec[g] + W, 0:256], P[SEQ - W:SEQ, 768 + 256 * g:768 + 256 * g + 256])
                em.dma(XB[xsec[g]:xsec[g] + W, 256:512], P[SEQ - W:SEQ, 1536 + 256 * g:1536 + 256 * g + 256])
            em.dma(XB[2688:2704, :], P[SEQ - 16:SEQ, 2304:2816])
            for j in range(3):
                em.collective(XB[1024 * j:1024 * (j + 1), :], XGs[j][:, :])

            for g, (W, d) in enumerate(GROUPS):
                em.dma(kvp[g][l, :, 0:256], P[SEQ - W:SEQ, 768 + 256 * g:768 + 256 * g + 256])
                em.dma(kvp[g][l, :, 256:512], P[SEQ - W:SEQ, 1536 + 256 * g:1536 + 256 * g + 256])
                for s in range(4):
                    em.dma(kvs[g][l, s, W - 8:W, 0:256], P[SEQ + 8 * s:SEQ + 8 * s + 8, 768 + 256 * g:768 + 256 * g + 256])
                    em.dma(kvs[g][l, s, W - 8:W, 256:512], P[SEQ + 8 * s:SEQ + 8 * s + 8, 1536 + 256 * g:1536 + 256 * g + 256])
            em.dma(poolp[l, :, :], P[SEQ - 15:SEQ, 2304:2816])
            for s in range(4):
                em.dma(pools[l, s, 0:7, :], spool[l, s, 8:15, :])
                em.dma(pools[l, s, 7:15, :], P[SEQ + 8 * s:SEQ + 8 * s + 8, 2304:2816])

            psCs = [psC, psB]
            cblocks = []
            for g, (W, d) in enumerate(GROUPS):
                nbc = SEQ // d // 128
                for r in range(d):
                    for qb in range(nbc):
                        cblocks.append((g, d, r, qb))

            def c_pre(n):
                g, d, r, qb = cblocks[n]
                a0 = r + d * 128 * qb
                a1 = a0 + 127 * d + 1
                bp = n & 1
                cur = n % 3
                KTx, VEx = KTxs[bp], VExs[bp]
                qkv, qkb, QT = qkvs[bp], sm1s[bp][:, 0:512], QTs[bp]
                em.dma(qkv[:, 0:256], P[a0:a1:d, 256 * g:256 * g + 256])
                em.dma(qkv[:, 256:512], P[a0:a1:d, 768 + 256 * g:768 + 256 * g + 256])
                em.dma(qkv[:, 512:768], P[a0:a1:d, 1536 + 256 * g:1536 + 256 * g + 256])
                if qb == 0:
                    prevt = FA[:, 1536:2048] if (n & 1) == 0 else FA[:, 2048:2560]
                    pkb = sm1s[bp][:, 512:768]
                    if g == 2:
                        em.dma(prevt[0:64, :], XGs[0][r:r + 16 * 63 + 1:16, :])
                        em.dma(prevt[64:128, :], XGs[1][r:r + 16 * 63 + 1:16, :])
                    elif g == 1:
                        em.dma(prevt[:, :], XGs[2][r:r + 4 * 127 + 1:4, :])
                    else:
                        em.dma(prevt[:, :], XGs[2][512:640, :])
                    V("tensor_copy", out=pkb, in_=prevt[:, 0:256])
                    G("tensor_scalar", out=VEx[:, :, 0:64], in0=prevt[:, 256:512].rearrange("p (h e) -> p h e", h=4),
                      scalar1=flag[:, 0:1], scalar2=None, op0=ALU.mult)
                    psTp = next_psT()
                    tpv = psTp[:, 0:512].rearrange("p (c t) -> p c t", c=4)
                    for k in range(4):
                        T("transpose", tpv[0:64, k, :], pkb[:, 64 * k:64 * k + 64], ident_b[:, :])
                    A("copy", out=KTx[:, :, :], in_=tpv[0:64, :, :])
                    ktp = KTx
                else:
                    ktp = KT[(n - 1) % 3]
                V("tensor_copy", out=qkb, in_=qkv[:, 0:512])
                G("tensor_copy", out=VE[cur][:, :, 0:64], in_=qkv[:, 512:768].rearrange("p (h e) -> p h e", h=4))
                psT = next_psT()
                tqv = psT[:, 0:1024].rearrange("p (c t) -> p c t", c=8)
                for k in range(8):
                    T("transpose", tqv[0:64, k, :], qkb[:, 64 * k:64 * k + 64], ident_b[:, :])
                V("tensor_copy", out=QT[:, :, :], in_=tqv[0:64, 0:4, :])
                A("copy", out=KT[cur][:, :, :], in_=tqv[0:64, 4:8, :])
                for hp in range(2):
                    psD = psDs[hp]
                    pt = sm2s[bp][:, 512 * hp:512 * hp + 512]
                    for hh in range(2):
                        h = 2 * hp + hh
                        o = 256 * hh
                        T("matmul", psD[:, o:o + 128], lhsT=ktp[:, h, :], rhs=QT[:, h, :], start=True, stop=False)
                        T("matmul", psD[:, o:o + 128], lhsT=ident_b[:, :], rhs=mask_b[:, 0:128], start=False, stop=True)
                        T("matmul", psD[:, o + 128:o + 256], lhsT=KT[cur][:, h, :], rhs=QT[:, h, :], start=True, stop=False)
                        T("matmul", psD[:, o + 128:o + 256], lhsT=ident_b[:, :], rhs=mask_b[:, 128:256], start=False, stop=True)
                    A("activation", out=pt, in_=psD[:, 0:512], func=AF.Exp, scale=0.125)

            def c_post(n):
                g, d, r, qb = cblocks[n]
                a0 = r + d * 128 * qb
                a1 = a0 + 127 * d + 1
                bp = n & 1
                cur = n % 3
                og = ogs[bp]
                vep = VExs[bp] if qb == 0 else VE[(n - 1) % 3]
                pov = psCs[bp][:, 0:260].rearrange("p (h e) -> p h e", h=4)
                for hp in range(2):
                    pt = sm2s[bp][:, 512 * hp:512 * hp + 512]
                    for hh in range(2):
                        h = 2 * hp + hh
                        o = 256 * hh
                        T("matmul", pov[:, h, :], lhsT=pt[:, o:o + 128], rhs=vep[:, h, :], start=True, stop=False)
                        T("matmul", pov[:, h, :], lhsT=pt[:, o + 128:o + 256], rhs=VE[cur][:, h, :], start=False, stop=True)
                V("tensor_copy", out=og[:, :], in_=psCs[bp][:, 0:260])
                em.dma(OG[g, a0:a1:d, :], og[:, :])

            def stage_C():
                c_pre(0)
                for n in range(len(cblocks)):
                    if n + 1 < len(cblocks):
                        c_pre(n + 1)
                    bg_step(2)
                    c_post(n)

            qs = FA[0:32, 4096:5632]
            qsb = sm1s[0][0:32, 0:768]
            ksb = sm2s[0][0:32, 0:768]
            em.dma(qs, P[SEQ:NTOK, 0:1536])
            V("tensor_copy", out=qsb, in_=qs[:, 0:768])
            V("tensor_copy", out=ksb, in_=qs[:, 768:1536])
            psT = next_psT()
            t12 = psT[:, 0:384].rearrange("p (c t) -> p c t", c=12)
            for k in range(12):
                T("transpose", t12[0:64, k, :], qsb[:, 64 * k:64 * k + 64], ident_b[0:32, 0:32])
            V("tensor_copy", out=QTS[:, :, :], in_=t12[0:64, :, :])
            psT = next_psT()
            t12 = psT[:, 0:384].rearrange("p (c t) -> p c t", c=12)
            for k in range(12):
                T("transpose", t12[0:64, k, :], ksb[:, 64 * k:64 * k + 64], ident_b[0:32, 0:32])
            V("tensor_copy", out=KTS[:, :, :], in_=t12[0:64, :, :])
            vsn = FA[0:8, 5632:6400]
            ct4s = [FA[:, 0:2048].rearrange("p (b c) -> p b c", b=4), FA[:, 2048:4096].rearrange("p (b c) -> p b c", b=4)]
            kt4s = [WBIG[0:64, 0:2048].rearrange("p (c t) -> p c t", c=16),
                    WBIG[0:64, 2048:4096].rearrange("p (c t) -> p c t", c=16)]
            ve4s = [WBIG[:, 4096:5136].rearrange("p (b h e) -> p b h e", b=4, h=4),
                    WBIG[:, 5136:6176].rearrange("p (b h e) -> p b h e", b=4, h=4)]
            ckb4s = [sm1s[0][:, 0:1024].rearrange("p (b c) -> p b c", b=4), sm1s[1][:, 0:1024].rearrange("p (b c) -> p b c", b=4)]
            for i in range(2):
                V("memset", ve4s[i], 1.0)
            cb = 0
            for s in range(4):
                acc = accs[s & 1]
                rd = rds[s & 1]
                oas = junks[s & 1][0:8, 0:256]
                em.dma(vsn, P[SEQ + 8 * s:SEQ + 8 * s + 8, 1536:2304])
                V("tensor_copy", out=VES[:, :, 0:64], in_=vsn.rearrange("p (h e) -> p h e", h=12))
                V("memset", acc[0:8], 0.0)
                for g, (W, d) in enumerate(GROUPS):
                    nblk = W // 128
                    for b0 in range(0, nblk, 4):
                        nb = min(4, nblk - b0)
                        bp = cb & 1
                        cb += 1
                        bg_step(2)
                        ct, ckb, kt, ve, pts = ct4s[bp], ckb4s[bp], kt4s[bp], ve4s[bp], sm2s[bp][:, 0:128]
                        em.dma(ct[:, 0:nb, :], caches[g][l, s, 128 * b0:128 * (b0 + nb), :].rearrange("(b p) c -> p b c", p=128))
                        V("tensor_copy", out=ckb[:, 0:nb, :], in_=ct[:, 0:nb, 0:256])
                        G("tensor_copy", out=ve[:, 0:nb, :, 0:64], in_=ct[:, 0:nb, 256:512].rearrange("p b (h e) -> p b h e", h=4))
                        for b in range(nb):
                            tkv = psTs[b // 2][:, 512 * (b % 2):512 * (b % 2) + 512].rearrange("p (c t) -> p c t", c=4)
                            for h in range(4):
                                T("transpose", tkv[0:64, h, :], ckb[:, b, 64 * h:64 * h + 64], ident_b[:, :])
                        for half in range((nb + 1) // 2):
                            nbh = min(2, nb - 2 * half)
                            src = psTs[half][:, 0:512 * nbh].rearrange("p (c t) -> p c t", c=4 * nbh)
                            if half == 0:
                                A("copy", out=kt[:, 0:4 * nbh, :], in_=src[0:64, :, :])
                            else:
                                V("tensor_copy", out=kt[:, 8:8 + 4 * nbh, :], in_=src[0:64, :, :])
                        psD = psDs[bp]
                        pssv = psD[:, 0:128].rearrange("p (c q) -> p c q", c=16)
                        posv = psD[:, 128:388].rearrange("p (h e) -> p h e", h=4)
                        for b in range(nb):
                            for h in range(4):
                                T("matmul", pssv[:, 4 * b + h, :], lhsT=kt[:, 4 * b + h, :],
                                  rhs=QTS[:, 4 * g + h, 8 * s:8 * s + 8], start=True, stop=True)
                        A("activation", out=pts[:, 0:32 * nb], in_=psD[:, 0:32 * nb], func=AF.Exp, scale=0.125)
                        mv = 2 * g + (1 if b0 > 0 else 0)
                        V("tensor_tensor", out=pts[:, 0:32 * nb], in0=pts[:, 0:32 * nb], in1=smask_b[:, mv, 0:32 * nb], op=ALU.mult)
                        for h in range(4):
                            for b in range(nb):
                                T("matmul", posv[0:8, h, :], lhsT=pts[:, 32 * b + 8 * h:32 * b + 8 * h + 8], rhs=ve[:, b, h, :],
                                  start=(b == 0), stop=(b == nb - 1))
                        V("tensor_tensor", out=acc[0:8], in0=acc[0:8], in1=posv[0:8], op=ALU.add)
                bp = cb & 1
                cb += 1
                psD = psDs[bp]
                ptn = sm2s[bp][:, 128:224]
                pssn = psD[:, 0:96].rearrange("p (c q) -> p c q", c=12)
                posv = psD[:, 128:388].rearrange("p (h e) -> p h e", h=4)
                for c12 in range(12):
                    T("matmul", pssn[0:8, c12, :], lhsT=KTS[:, c12, 8 * s:8 * s + 8],
                      rhs=QTS[:, c12, 8 * s:8 * s + 8], start=True, stop=True)
                A("activation", out=ptn[0:8], in_=psD[0:8, 0:96], func=AF.Exp, scale=0.125)
                V("tensor_tensor", out=ptn[0:8], in0=ptn[0:8], in1=snew_b[:, :], op=ALU.mult)
                for h in range(4):
                    for g in range(3):
                        T("matmul", posv[0:8, h, :], lhsT=ptn[0:8, 8 * (4 * g + h):8 * (4 * g + h) + 8],
                          rhs=VES[0:8, 4 * g + h, :], start=(g == 0), stop=(g == 2))
                V("tensor_tensor", out=acc[0:8], in0=acc[0:8], in1=posv[0:8], op=ALU.add)
                V("reciprocal", out=rd[0:8], in_=acc[0:8, :, 64:65])
                V("tensor_tensor", out=oas.rearrange("p (h e) -> p h e", h=4), in0=acc[0:8, :, 0:64],
                  in1=rd[0:8].to_broadcast([8, 4, 64]), op=ALU.mult)
                em.dma(OA[8 * s:8 * s + 8, :], oas)

            LIN = WBIG[:, 0:512].rearrange("p (c d) -> p c d", c=4)
            for c in range(4):
                load_weight(LIN[:, c, :], plin[l, c, :, :], 128, 128)
            em.dma(pscale[:], pscaleT[l, :, :])
            HS = SEQ // 2
            NJ = HS + 16
            sptx = junks[0][0:16, 0:512]
            em.dma(sptx, XGs[2][640:656, :])
            ue = FA[:, 0:NJ]
            sA = FA[:, NJ:2 * NJ]
            sB = FA[:, 2 * NJ:3 * NJ]
            pbts = [WBIG[:, 2048:2048 + HS], WBIG[:, 2048 + HS:2048 + 2 * HS]]
            obts = [WBIG[:, 2048 + 2 * HS:2048 + 3 * HS], WBIG[:, 2048 + 3 * HS:2048 + 4 * HS]]
            for c, w in enumerate(POOLW):
                for hh in range(2):
                    pbt, obt = pbts[hh], obts[hh]
                    t0 = HS * hh
                    if hh == 0:
                        T("transpose", psC[:, 64:80], sptx[0:16, 128 * c:128 * c + 128], ident_f[0:16, 0:16])
                        V("tensor_scalar", out=ue[:, 0:16], in0=psC[:, 64:80], scalar1=flag[:, 0:1], scalar2=None,
                          op0=ALU.mult)
                    else:
                        em.dma(ue[:, 0:16], UT[c, :, t0 - 16:t0])
                    em.dma(ue[:, 16:NJ], UT[c, :, t0:t0 + HS])
                    cur = ue
                    step = 1
                    bufs = [sA, sB]
                    bi = 0
                    while step < w:
                        lo = 2 * step - 1
                        nxt = bufs[bi]
                        bi ^= 1
                        V("tensor_tensor", out=nxt[:, lo:NJ], in0=cur[:, lo:NJ], in1=cur[:, lo - step:NJ - step], op=ALU.add)
                        cur = nxt
                        step *= 2
                    V("scalar_tensor_tensor", out=pbt, in0=cur[:, 16:NJ], scalar=1.0 / w,
                      in1=ue[:, 16:NJ], op0=ALU.mult, op1=ALU.subtract)
                    if hh == 0:
                        V("tensor_tensor", out=fix16[:, :], in0=cur[:, 16:32], in1=rcfix[:, c, :], op=ALU.mult)
                        V("tensor_tensor", out=pbt[:, 0:16], in0=fix16[:, :], in1=ue[:, 16:32], op=ALU.subtract)
                    for tt in range(HS // 512):
                        psA = next_psA()
                        T("matmul", psA[:, 0:512], lhsT=LIN[:, c, :], rhs=pbt[:, 512 * tt:512 * tt + 512],
                          start=True, stop=True)
                        V("tensor_scalar", out=obt[:, 512 * tt:512 * tt + 512], in0=psA[:, 0:512],
                          scalar1=pscale[:, c:c + 1], scalar2=None, op0=ALU.mult)
                    em.dma(OBT[c, :, t0:t0 + HS], obt)
                xs = xts[c & 1]
                ues = xs[:, 0:96].rearrange("p (s j) -> p s j", s=4)
                sAs = xs[:, 96:192].rearrange("p (s j) -> p s j", s=4)
                sBs = xs[:, 192:288].rearrange("p (s j) -> p s j", s=4)
                V("memset", ues[:, :, 0:1], 0.0)
                for s in range(4):
                    spt = sss_big[s & 1][0:15, 0:512]
                    em.dma(spt, spool[l, s, :, :])
                    T("transpose", psC[:, 16 * s:16 * s + 15], spt[0:15, 128 * c:128 * c + 128], ident_f[0:15, 0:15])
                    V("tensor_copy", out=ues[:, s, 1:16], in_=psC[:, 16 * s:16 * s + 15])
                em.dma(ues[:, :, 16:24], UT[c, :, SEQ:NTOK].rearrange("f (s t) -> f s t", s=4))
                cur = ues
                step = 1
                bufs = [sAs, sBs]
                bi = 0
                while step < w:
                    lo = 2 * step - 1
                    nxt = bufs[bi]
                    bi ^= 1
                    V("tensor_tensor", out=nxt[:, :, lo:24], in0=cur[:, :, lo:24], in1=cur[:, :, lo - step:24 - step], op=ALU.add)
                    cur = nxt
                    step *= 2
                sm1, sm2 = sm1s[c & 1], sm2s[c & 1]
                pbs = sm1[:, 0:32].rearrange("p (s t) -> p s t", s=4)
                V("scalar_tensor_tensor", out=pbs, in0=cur[:, :, 16:24], scalar=1.0 / w,
                  in1=ues[:, :, 16:24], op0=ALU.mult, op1=ALU.subtract)
                psA = next_psA()
                T("matmul", psA[:, 0:32], lhsT=LIN[:, c, :], rhs=sm1[:, 0:32], start=True, stop=True)
                V("tensor_scalar", out=sm2[:, 0:32], in0=psA[:, 0:32], scalar1=pscale[:, c:c + 1],
                  scalar2=None, op0=ALU.mult)
                em.dma(OBT[c, :, SEQ:NTOK], sm2[:, 0:32])

            stage_C()

            WPA = WBIG[0:64, 0:4096].rearrange("p (h c) -> p h c", h=4)
            WPB = WBIG[:, 4096:8192].rearrange("p (k c) -> p k c", k=4)
            WO = WBIG[:, 8192:16384].rearrange("p (k c) -> p k c", k=8)
            if l == 0:
                for h in range(4):
                    load_weight(WPA[:, h, :], w_pa[l, 64 * h:64 * h + 64, :], 64, D)
                for k in range(4):
                    load_weight(WPB[:, k, :], w_pb[l, 128 * k:128 * k + 128, :], 128, D)
                for k in range(8):
                    load_weight(WO[:, k, :], w_o[l, 128 * k:128 * k + 128, :], 128, D)
            else:
                em.dma(WBIG[0:64, 0:4096], WBpa[wp][:, :])
                em.dma(WBIG[:, 4096:8192], WBpb[wp][:, :])
                em.dma(WBIG[:, 8192:12288], WBo[wp][:, 0:4096])
                em.dma(WBIG[:, 12288:16384], WBo[wp][:, 4096:8192])
            mTs = [WBIG[:, 16384:17408].rearrange("p (k t) -> p k t", k=8),
                   WBIG[:, 17408:18432].rearrange("p (k t) -> p k t", k=8)]
            def m_bufs(tb):
                pr = tb & 1
                rows = 128 if tb < NPB else NSMP
                return (rows, tb * 128, xts[pr], junks[pr], hTs[pr], accs[pr], acc2s[pr], acc3s[pr], rds[pr],
                        sm1s[pr][:, 0:256], sm2s[pr][:, 0:1024], oaTs[pr], mTs[pr],
                        projs[pr][:, 0:2048], projs[pr][:, 2048:3072], projs[pr][:, 3072:4096])

            def m_pre(tb):
                rows, r0, xt, junk, hT, acc, acc2, acc3, rd, oab, mxb, oaT, mT, gt, mx, tmpm = m_bufs(tb)
                if tb < NPB:
                    em.dma(acc[:].rearrange("p h e -> p (h e)"), OG[0, r0:r0 + 128, :])
                    em.dma(acc2[:].rearrange("p h e -> p (h e)"), OG[1, r0:r0 + 128, :])
                    em.dma(acc3[:].rearrange("p h e -> p (h e)"), OG[2, r0:r0 + 128, :])
                    G("tensor_tensor", out=acc[:], in0=acc[:], in1=acc2[:], op=ALU.add)
                    G("tensor_tensor", out=acc[:], in0=acc[:], in1=acc3[:], op=ALU.add)
                    V("reciprocal", out=rd[:], in_=acc[:, :, 64:65])
                    V("tensor_tensor", out=oab.rearrange("p (h e) -> p h e", h=4), in0=acc[:, :, 0:64],
                      in1=rd[:].to_broadcast([128, 4, 64]), op=ALU.mult)
                else:
                    em.dma(junk[0:32, 0:256], OA[:, :])
                    V("tensor_copy", out=oab[0:32], in_=junk[0:32, 0:256])
                em.dma(hT[:, 0:4, :rows], OBT.rearrange("c f t -> f c t")[:, :, r0:r0 + rows])
                em.dma(gt[:rows], P[r0:r0 + rows, 2816:INW])
                em.dma(xt[:rows], xsrc[r0:r0 + rows, :])
                transposes_bf(oab, rows, 4, 64, oaT)

            def m_mm1(tb):
                rows, r0, xt, junk, hT, acc, acc2, acc3, rd, oab, mxb, oaT, mT, gt, mx, tmpm = m_bufs(tb)
                for ng in range(2):
                    psA = next_psA()
                    for h in range(4):
                        T("matmul", psA[:rows, 0:512], lhsT=oaT[:, h, :rows], rhs=WPA[:, h, 512 * ng:512 * ng + 512],
                          start=(h == 0), stop=(h == 3))
                    V("tensor_tensor", out=mx[:rows, 512 * ng:512 * ng + 512], in0=psA[:rows, 0:512],
                      in1=gt[:rows, 512 * ng:512 * ng + 512], op=ALU.mult)
                    psA = next_psA()
                    for k in range(4):
                        T("matmul", psA[:rows, 0:512], lhsT=hT[:, k, :rows], rhs=WPB[:, k, 512 * ng:512 * ng + 512],
                          start=(k == 0), stop=(k == 3))
                    V("tensor_tensor", out=tmpm[:rows, 512 * ng:512 * ng + 512], in0=psA[:rows, 0:512],
                      in1=gt[:rows, 1024 + 512 * ng:1024 + 512 * ng + 512], op=ALU.mult)
                G("tensor_tensor", out=mxb[:rows], in0=mx[:rows], in1=tmpm[:rows], op=ALU.add)

            def m_T2(tb):
                rows, r0, xt, junk, hT, acc, acc2, acc3, rd, oab, mxb, oaT, mT, gt, mx, tmpm = m_bufs(tb)
                transposes_bf(mxb, rows, 8, 128, mT)

            def m_mm2(tb):
                rows, r0, xt, junk, hT, acc, acc2, acc3, rd, oab, mxb, oaT, mT, gt, mx, tmpm = m_bufs(tb)
                for ng in range(2):
                    psA = next_psA()
                    for k in range(8):
                        T("matmul", psA[:rows, 0:512], lhsT=mT[:, k, :rows], rhs=WO[:, k, 512 * ng:512 * ng + 512],
                          start=(k == 0), stop=(k == 7))
                    V("tensor_tensor", out=junk[:rows, 512 * ng:512 * ng + 512], in0=psA[:rows, 0:512],
                      in1=xt[:rows, 512 * ng:512 * ng + 512], op=ALU.add)
                em.dma(X[r0:r0 + rows, :], junk[:rows, :])

            m_pre(0)
            m_mm1(0)
            for tb in range(NBLK):
                if tb + 1 < NBLK:
                    m_pre(tb + 1)
                bg_step(2)
                m_T2(tb)
                if tb + 1 < NBLK:
                    m_mm1(tb + 1)
                m_mm2(tb)

            em.dma(gbc[:], norm2[l:l + 1, :].partition_broadcast(128))
            WUP = WBIG[:, 0:16384].rearrange("p (k c) -> p k c", k=8)
            WDN = WBIG[:, 16384:32768].rearrange("p (k c) -> p k c", k=16)
            abs_ = [WBIG[:, 32768:34816], WBIG[:, 34816:36864]]
            aTs = [WBIG[:, 36864:38912].rearrange("p (k t) -> p k t", k=16),
                   WBIG[:, 38912:40960].rearrange("p (k t) -> p k t", k=16)]
            for hf in range(2):
                if l == 0:
                    for k in range(8):
                        load_weight(WUP[:, k, :], w_up[l, 128 * k:128 * k + 128, 2048 * hf:2048 * hf + 2048], 128, 2048)
                    for k in range(16):
                        load_weight(WDN[:, k, :], w_down[l, 2048 * hf + 128 * k:2048 * hf + 128 * k + 128, :], 128, D)
                else:
                    for j4 in range(4):
                        em.dma(WBIG[:, 4096 * j4:4096 * (j4 + 1)], WBup[wp][hf][:, 4096 * j4:4096 * (j4 + 1)])
                    for j4 in range(4):
                        em.dma(WBIG[:, 16384 + 4096 * j4:16384 + 4096 * (j4 + 1)], WBdn[wp][hf][:, 4096 * j4:4096 * (j4 + 1)])
                def d_bufs(tb):
                    pr = tb & 1
                    rows = 128 if tb < NPB else NSMP
                    return (rows, tb * 128, xts[pr], junks[pr], sss[pr], hbs[pr], hTs[pr], abs_[pr], aTs[pr],
                            projs[0][:, 1024 * pr:1024 * pr + 1024], projs[1][:, 1024 * pr:1024 * pr + 1024])

                def d_pre(tb):
                    rows, r0, xt, junk, ss, hb, hT, ab, aT, x2t, outb = d_bufs(tb)
                    em.dma(xt[:rows], X[r0:r0 + rows, :])
                    if hf == 1:
                        em.dma(x2t[:rows], X2[r0:r0 + rows, :])
                    rmsnorm_rows(xt, junk, ss, rows)
                    V("scalar_tensor_tensor", out=hb[:rows], in0=xt[:rows], scalar=ss[:rows, 3:4],
                      in1=gbc[:rows], op0=ALU.mult, op1=ALU.mult)
                    transposes_bf(hb, rows, 8, 128, hT)

                def d_up(tb):
                    rows, r0, xt, junk, ss, hb, hT, ab, aT, x2t, outb = d_bufs(tb)
                    rls = [projs[0][:, 2048:2560], projs[0][:, 2560:3072]]
                    for ng in range(4):
                        psA = next_psA()
                        rl = rls[ng & 1]
                        for k in range(8):
                            T("matmul", psA[:rows, 0:512], lhsT=hT[:, k, :rows], rhs=WUP[:, k, 512 * ng:512 * ng + 512],
                              start=(k == 0), stop=(k == 7))
                        A("activation", out=rl[:rows], in_=psA[:rows, 0:512], func=AF.Relu)
                        G("tensor_tensor", out=ab[:rows, 512 * ng:512 * ng + 512], in0=rl[:rows], in1=rl[:rows], op=ALU.mult)

                def d_abT(tb):
                    rows, r0, xt, junk, ss, hb, hT, ab, aT, x2t, outb = d_bufs(tb)
                    for j in range(2):
                        psT = next_psT()
                        pv = psT[:, 0:1024].rearrange("p (c t) -> p c t", c=8)
                        for k in range(8):
                            T("transpose", pv[:, k, :rows], ab[:rows, (8 * j + k) * 128:(8 * j + k + 1) * 128],
                              ident_b[:rows, :rows])
                        if j == 0:
                            V("tensor_copy", out=aT[:, 8 * j:8 * j + 8, :rows], in_=pv[:, :, :rows])
                        else:
                            A("copy", out=aT[:, 8 * j:8 * j + 8, :rows], in_=pv[:, :, :rows])

                def d_down(tb):
                    rows, r0, xt, junk, ss, hb, hT, ab, aT, x2t, outb = d_bufs(tb)
                    res_src = xt if hf == 0 else x2t
                    for ng in range(2):
                        psA = next_psA()
                        for k in range(16):
                            T("matmul", psA[:rows, 0:512], lhsT=aT[:, k, :rows], rhs=WDN[:, k, 512 * ng:512 * ng + 512],
                              start=(k == 0), stop=(k == 15))
                        V("tensor_tensor", out=outb[:rows, 512 * ng:512 * ng + 512], in0=psA[:rows, 0:512],
                          in1=res_src[:rows, 512 * ng:512 * ng + 512], op=ALU.add)
                    em.dma((X2 if hf == 0 else X)[r0:r0 + rows, :], outb[:rows, :])

                d_pre(0)
                d_up(0)
                for tb in range(NBLK):
                    if tb + 1 < NBLK:
                        d_pre(tb + 1)
                    bg_step(2)
                    d_abT(tb)
                    if tb + 1 < NBLK:
                        d_up(tb + 1)
                    d_down(tb)

        em.dma(gbc[:], fnorm[0:1, :].partition_broadcast(128))
        for tb in range(NBLK):
            rows = 128 if tb < NPB else NSMP
            r0 = tb * 128
            pr = tb & 1
            xt, junk, ss = xts[pr], junks[pr], sss[pr]
            yo = proj[:, 1024 * pr:1024 * pr + 1024]
            em.dma(xt[:rows], X[r0:r0 + rows, :])
            rmsnorm_rows(xt, junk, ss, rows)
            V("scalar_tensor_tensor", out=yo[:rows], in0=xt[:rows], scalar=ss[:rows, 3:4],
              in1=gbc[:rows], op0=ALU.mult, op1=ALU.mult)
            em.dma(y[r0:r0 + rows, :], yo[:rows, :])
        em.flush()

        with nc.Block() as block:
            @block.sync
            def _(e):
                em.replay("sync", e)

            @block.tensor
            def _(e):
                em.replay("tensor", e)

            @block.vector
            def _(e):
                em.replay("vector", e)

            @block.scalar
            def _(e):
                em.replay("scalar", e)

            @block.gpsimd
            def _(e):
                em.replay("gpsimd", e)
    return nc


def _tables():
    pos = np.concatenate([np.arange(FULLSEQ), np.tile(8192 + np.arange(8), 4)]).astype(np.float32)
    inv = (500000.0 ** (-np.arange(0, 16, 2, dtype=np.float32) / 16)).astype(np.float32)
    ang = pos[:, None] * inv[None, :]
    cstab = np.concatenate([np.cos(ang), np.sin(ang)], axis=1).astype(np.float32)
    i = np.arange(128)[:, None]
    j = np.arange(128)[None, :]
    maskpc = np.where(np.concatenate([(i >= j), (i <= j)], axis=1), 0.0, -30000.0).astype(np.float32)
    smask = np.zeros((3, 2, 128, 32), np.float32)
    snew = np.zeros((3, 8, 32), np.float32)
    for g, (W, d) in enumerate(GROUPS):
        for v in range(2):
            for ii in range(128):
                for t in range(8):
                    ok = ((ii - t) % d == 0) and (v == 1 or ii >= t)
                    for h in range(4):
                        smask[g, v, ii, 8 * h + t] = 1.0 if ok else 0.0
        for tk in range(8):
            for t in range(8):
                ok = (tk <= t) and ((t - tk) % d == 0)
                for h in range(4):
                    snew[g, tk, 8 * h + t] = 1.0 if ok else 0.0
    rcfix = np.zeros((2, 4, 128, 16), np.float32)
    for c, w in enumerate(POOLW):
        for t in range(16):
            rcfix[0, c, :, t] = 1.0 / min(w, t + 1)
            rcfix[1, c, :, t] = 1.0 / w
    smask4 = np.zeros((3, 2, 128, 4, 32), np.float32)
    for g in range(3):
        for fb in range(2):
            for b in range(4):
                smask4[g, fb, :, b, :] = smask[g, 0 if (fb == 0 and b == 0) else 1]
    smask = smask4.reshape(3, 2, 128, 128)
    snew = np.ascontiguousarray(snew.transpose(1, 0, 2).reshape(8, 96))
    return cstab, maskpc, smask, snew, rcfix


def kernel(x_prompt, x_sample, cache_kv_w128, cache_kv_w512, cache_kv_w2048, state_pool,
           norm1, w_in, w_pa, w_pb, pool_lin, pool_scale, w_o, norm2, w_up, w_down, final_norm):
    f = lambda a: np.ascontiguousarray(np.asarray(a, dtype=np.float32))
    x_prompt, x_sample = f(x_prompt), f(x_sample)
    cch = [f(cache_kv_w128), f(cache_kv_w512), f(cache_kv_w2048)]
    state_pool = f(state_pool)
    cstab, maskpc, smask, snew, rcfix = _tables()
    shared = {
        "norm1": f(norm1), "norm2": f(norm2), "fnorm": f(final_norm).reshape(1, D),
        "w_in": f(w_in), "w_pa": f(w_pa), "w_pb": f(w_pb), "plin": f(pool_lin),
        "pscaleT": np.ascontiguousarray(f(pool_scale).reshape(DEPTH, 4, 128).transpose(0, 2, 1)),
        "w_o": f(w_o), "w_up": f(w_up), "w_down": f(w_down),
        "ident": np.eye(128, dtype=np.float32), "maskpc": maskpc,
        "smask": smask, "snew": snew,
    }
    in_maps = []
    for c in range(8):
        m = dict(shared)
        b, hf = c // 2, c % 2
        m["xin"] = np.ascontiguousarray(np.concatenate(
            [x_prompt[b, SEQ * hf:SEQ * hf + SEQ], x_sample[4 * c:4 * c + 4].reshape(NSMP, D)], axis=0))
        m["cstab"] = np.ascontiguousarray(np.concatenate([cstab[SEQ * hf:SEQ * hf + SEQ], cstab[FULLSEQ:]], axis=0))
        m["rcfix"] = np.ascontiguousarray(rcfix[hf])
        m["flag"] = np.full((128, 1), float(hf), np.float32)
        for g in range(3):
            W = GROUPS[g][0]
            m["cache%d" % g] = np.ascontiguousarray(cch[g][:, 4 * c:4 * c + 4].reshape(DEPTH, 4, W, 512))
        m["spool"] = np.ascontiguousarray(state_pool[:, 4 * c:4 * c + 4])
        in_maps.append(m)
    nc = build_program()
    res = run_bass_kernel_spmd(nc, in_maps, core_ids=list(range(8)))
    R = res.results
    y_prompt = np.stack([np.concatenate([R[2 * b]["y"][:SEQ], R[2 * b + 1]["y"][:SEQ]], axis=0) for b in range(4)], axis=0)
    y_sample = np.concatenate([R[c]["y"][SEQ:].reshape(4, 8, D) for c in range(8)], axis=0)
    outs = [y_prompt.astype(np.float32), y_sample.astype(np.float32)]
    for g in range(3):
        W = GROUPS[g][0]
        outs.append(np.stack([R[2 * b + 1]["kvp%d" % g].reshape(DEPTH, W, 2, 4, 64) for b in range(4)], axis=1))
    outs.append(np.stack([R[2 * b + 1]["poolp"] for b in range(4)], axis=1))
    for g in range(3):
        W = GROUPS[g][0]
        outs.append(np.concatenate([R[c]["kvs%d" % g].reshape(DEPTH, 4, W, 2, 4, 64) for c in range(8)], axis=1))
    outs.append(np.concatenate([R[c]["pools"] for c in range(8)], axis=1))
    return tuple(np.ascontiguousarray(o, dtype=np.float32) for o in outs)
```

```python
from contextlib import ExitStack
import numpy as np
import concourse.bass as bass
import concourse.mybir as mybir
from concourse.bass_utils import run_bass_kernel_spmd

F32 = mybir.dt.float32
BF16 = mybir.dt.bfloat16
ALU = mybir.AluOpType
AF = mybir.ActivationFunctionType

D = 1024
FULLSEQ = 4096
SEQ = 2048
NSMP = 32
NTOK = SEQ + NSMP
NBLK = 17
NPB = 16
PAIRS = [[0, 1], [2, 3], [4, 5], [6, 7]]
DEPTH = 4
INW = 4864
GROUPS = ((128, 1), (512, 4), (2048, 16))
POOLW = (2, 4, 8, 16)
SEM_LIMIT = 30000
NSLOT = 32
SAME_ENGINE_WAIT = True
ENGS = ("sync", "tensor", "vector", "scalar", "gpsimd")


def _is_ap(x):
    return hasattr(x, "ap") and hasattr(x, "offset") and hasattr(x, "space")


def _is_dram(ap):
    return "DRAM" in str(ap.space).upper()


def _region(ap):
    isz = 4 if ap.dtype == F32 else 2
    dims = ap.ap
    off = ap.offset
    if _is_dram(ap):
        ext = sum(abs(s) * (c - 1) for s, c in dims)
        return (ap.name, 0, 1, off * isz, (off + ext + 1) * isz)
    if "PSUM" in str(ap.space).upper():
        return (ap.name, 0, 128, 0, 1 << 30)
    pstep, pcount = dims[0]
    p0 = off // pstep if pstep > 0 else 0
    f0 = off - p0 * pstep
    ext = sum(abs(s) * (c - 1) for s, c in dims[1:])
    return (ap.name, p0, p0 + pcount, f0 * isz, (f0 + ext + 1) * isz)


def _ovl(a, b):
    return a[1] < b[2] and b[1] < a[2] and a[3] < b[4] and b[3] < a[4]


def _contains(a, b):
    return a[1] <= b[1] and b[2] <= a[2] and a[3] <= b[3] and b[4] <= a[4]


class Em:
    def __init__(self, sems):
        self.sems = sems
        self.next_sem = 0
        self.cur = {}
        self.streams = {k: [] for k in ENGS}
        self.recs = {}
        self.known = {k: {} for k in ENGS}
        self.slots = {"sync": [None] * NSLOT, "scalar": [None] * 8}
        self.slot_n = {"sync": 0, "scalar": 0}
        self.pending = []
        self.seen_load = False
        self.nops = 0

    def _new_sem(self):
        i = self.next_sem
        self.next_sem += 1
        assert self.next_sem <= len(self.sems), "out of semaphores"
        return i

    def _alloc(self, eng):
        if eng not in self.cur or self.cur[eng][1] + 1 > SEM_LIMIT:
            self.cur[eng] = [self._new_sem(), 0]
        st = self.cur[eng]
        st[1] += 1
        return st[0], st[1]

    def _deps(self, eng, reads, writes, ref):
        deps = []
        for r in reads:
            lst = self.recs.setdefault(r[0], [])
            for rec in lst:
                if rec[1] and _ovl(rec[0], r):
                    deps.append(rec[2])
            lst[:] = [rec for rec in lst if not ((not rec[1]) and rec[2][3] == eng and rec[2][0] == ref[0]
                                                 and _contains(r, rec[0]))]
            lst.append((r, False, ref))
        for w in writes:
            lst = self.recs.setdefault(w[0], [])
            keep = []
            for rec in lst:
                if _ovl(rec[0], w):
                    if rec[2] is not ref:
                        deps.append(rec[2])
                    if _contains(w, rec[0]):
                        continue
                keep.append(rec)
            keep.append((w, True, ref))
            self.recs[w[0]] = keep
        return deps

    def _waits(self, eng, deps, is_dma):
        waits = []
        kn = self.known[eng]
        for kind, si, val, deng in deps:
            if kind == "c" and deng == eng and not is_dma and (eng == "tensor" or not SAME_ENGINE_WAIT):
                continue
            if kn.get(si, 0) >= val:
                continue
            kn[si] = val
            waits.append((si, val))
        return waits

    def flush(self):
        for deps, out, in_, si, ref in self.pending:
            waits = self._waits("sync", deps, True)
            self.streams["sync"].append((waits, "dma_start", (), {"out": out, "in_": in_}, si, 16))
        self.pending = []
        self.seen_load = False

    def op(self, eng, meth, a, k):
        if self.pending and self.seen_load:
            self.flush()
        outs = []
        if "out" in k:
            outs.append(k["out"])
        else:
            outs.append(a[0])
        if k.get("accum_out", None) is not None:
            outs.append(k["accum_out"])
        ins = [x for x in list(a[(0 if "out" in k else 1):]) if _is_ap(x)]
        ins += [v for kk, v in k.items() if kk not in ("out", "accum_out") and _is_ap(v)]
        si, val = self._alloc(eng)
        ref = ("c", si, val, eng)
        psum_ins = [x for x in ins if "PSUM" in str(x.space).upper()]
        ins = [x for x in ins if "PSUM" not in str(x.space).upper()]
        deps = self._deps(eng, [_region(x) for x in ins], [_region(x) for x in outs + psum_ins], ref)
        if self.pending:
            prefs = [p[4] for p in self.pending]
            if any(any(d is p for p in prefs) for d in deps):
                self.flush()
        waits = self._waits(eng, deps, False)
        self.streams[eng].append((waits, meth, a, k, si, 1))
        self.nops += 1

    def dma(self, out, in_, queue="sync"):
        is_store = _is_dram(out) and not _is_dram(in_) and queue == "sync"
        nsl = len(self.slots[queue])
        sl = self.slot_n[queue] % nsl
        self.slot_n[queue] += 1
        prev = self.slots[queue][sl]
        if prev is None:
            si, val = self._new_sem(), 16
        else:
            si, val = prev[1], prev[2] + 16
        ref = ("d", si, val, queue)
        deps = self._deps(queue, [_region(in_)], [_region(out)], ref)
        if prev is not None:
            deps.append(prev)
        self.slots[queue][sl] = ref
        self.nops += 1
        if queue != "sync":
            if self.pending:
                prefs = [p[4] for p in self.pending]
                if any(any(d is p for p in prefs) for d in deps):
                    self.flush()
            waits = self._waits(queue, deps, True)
            self.streams[queue].append((waits, "dma_start", (), {"out": out, "in_": in_}, si, 16))
            return
        if is_store:
            self.pending.append((deps, out, in_, si, ref))
            return
        if self.pending:
            prefs = [p[4] for p in self.pending]
            if any(any(d is p for p in prefs) for d in deps):
                self.flush()
        waits = self._waits("sync", deps, True)
        self.streams["sync"].append((waits, "dma_start", (), {"out": out, "in_": in_}, si, 16))
        self.seen_load = True

    def collective(self, in_ap, out_ap):
        if self.pending:
            self.flush()
        si = self._new_sem()
        ref = ("d", si, 1, "gpsimd")
        deps = self._deps("gpsimd", [_region(in_ap)], [_region(out_ap)], ref)
        waits = self._waits("gpsimd", deps, True)
        self.streams["gpsimd"].append((waits, "collective_compute", ("AllGather", ALU.bypass),
                                       dict(replica_groups=PAIRS, ins=[in_ap.opt()], outs=[out_ap.opt()]), si, None))
        self.nops += 1

    def replay(self, eng, handle):
        for waits, meth, a, k, si, inc in self.streams[eng]:
            for wsi, wval in waits:
                handle.wait_ge(self.sems[wsi], wval)
            inst = getattr(handle, meth)(*a, **k)
            if inc is None:
                inst.then_inc(self.sems[si])
            else:
                inst.then_inc(self.sems[si], inc)
        if eng == "sync":
            for q in self.slots:
                for ref in self.slots[q]:
                    if ref is not None:
                        handle.wait_ge(self.sems[ref[1]], ref[2])


def build_program():
    nc = bass.Bass("TRN2", target_bir_lowering=False)

    def din(name, shape, dt=F32):
        return nc.dram_tensor(name, list(shape), dt, kind="ExternalInput").ap()

    def dout(name, shape):
        return nc.dram_tensor(name, list(shape), F32, kind="ExternalOutput").ap()

    def dscr(name, shape, dt=F32):
        return nc.dram_tensor(name, list(shape), dt, kind="Internal").ap()

    xin = din("xin", [NTOK, D])
    caches = [din("cache%d" % g, [DEPTH, 4, GROUPS[g][0], 512]) for g in range(3)]
    spool = din("spool", [DEPTH, 4, 15, 512])
    norm1 = din("norm1", [DEPTH, D])
    norm2 = din("norm2", [DEPTH, D])
    fnorm = din("fnorm", [1, D])
    w_in = din("w_in", [DEPTH, D, INW])
    w_pa = din("w_pa", [DEPTH, 256, D])
    w_pb = din("w_pb", [DEPTH, 512, D])
    plin = din("plin", [DEPTH, 4, 128, 128])
    pscaleT = din("pscaleT", [DEPTH, 128, 4])
    w_o = din("w_o", [DEPTH, D, D])
    w_up = din("w_up", [DEPTH, D, 4 * D])
    w_down = din("w_down", [DEPTH, 4 * D, D])
    cstab = din("cstab", [NTOK, 16])
    ident_in = din("ident", [128, 128])
    mask_in = din("maskpc", [128, 256])
    smask_in = din("smask", [3, 2, 128, 128])
    snew_in = din("snew", [8, 96])
    rcfix_in = din("rcfix", [4, 128, 16])
    flag_in = din("flag", [128, 1])

    y = dout("y", [NTOK, D])
    kvp = [dout("kvp%d" % g, [DEPTH, GROUPS[g][0], 512]) for g in range(3)]
    poolp = dout("poolp", [DEPTH, 15, 512])
    kvs = [dout("kvs%d" % g, [DEPTH, 4, GROUPS[g][0], 512]) for g in range(3)]
    pools = dout("pools", [DEPTH, 4, 15, 512])

    X = dscr("X", [NTOK, D])
    X2 = dscr("X2", [NTOK, D])
    P = dscr("P", [NTOK, INW])
    UT = dscr("UT", [4, 128, NTOK])
    OBT = dscr("OBT", [4, 128, NTOK], BF16)
    OG = dscr("OG", [3, SEQ, 260])
    OA = dscr("OA", [NSMP, 256])
    XB = dscr("XB", [3072, 512])
    XGs = [dscr("XG%d" % j, [2048, 512]) for j in range(3)]

    with ExitStack() as es:
        sems = [es.enter_context(nc.semaphore("s%d" % i)) for i in range(90)]
        em = Em(sems)

        def sb(name, shape, dt=F32):
            return es.enter_context(nc.sbuf_tensor("sb_" + name, list(shape), dt))

        def ps(name, shape, dt=F32):
            return es.enter_context(nc.psum_tensor("ps_" + name, list(shape), dt))

        psAs = [ps("psA0", [128, 512]), ps("psA1", [128, 512])]
        psB = ps("psB", [128, 512])
        psC = ps("psC", [128, 512])
        psDs = [ps("psD0", [128, 512]), ps("psD1", [128, 512])]
        psTs = [ps("psT0", [128, 1024], BF16), ps("psT1", [128, 1024], BF16)]
        pa_n = [0]
        pt_n = [0]

        def next_psA():
            pa_n[0] += 1
            return psAs[pa_n[0] & 1]

        def next_psT():
            pt_n[0] += 1
            return psTs[pt_n[0] & 1]

        ident_f = sb("ident_f", [128, 128])
        ident_b = sb("ident_b", [128, 128], BF16)
        mask_f = sb("mask_f", [128, 256])
        mask_b = sb("mask_b", [128, 256], BF16)
        smask_f = sb("smask_f", [128, 6, 128])
        smask_b = sb("smask_b", [128, 6, 128], BF16)
        snew_f = sb("snew_f", [8, 96])
        snew_b = sb("snew_b", [8, 96], BF16)
        rcfix = sb("rcfix", [128, 4, 16])

        V = lambda m, *a, **k: em.op("vector", m, a, k)
        A = lambda m, *a, **k: em.op("scalar", m, a, k)
        G = lambda m, *a, **k: em.op("gpsimd", m, a, k)
        T = lambda m, *a, **k: em.op("tensor", m, a, k)

        em.dma(ident_f[:], ident_in[:, :])
        em.dma(mask_f[:], mask_in[:, :])
        em.dma(smask_f[:], smask_in.rearrange("g v p q -> p (g v) q"))
        em.dma(snew_f[:], snew_in[:, :])
        em.dma(rcfix[:], rcfix_in.rearrange("c p t -> p c t"))
        flag = sb("flag", [128, 1])
        em.dma(flag[:], flag_in[:, :])
        V("tensor_copy", out=ident_b[:], in_=ident_f[:])
        V("tensor_copy", out=mask_b[:], in_=mask_f[:])
        V("tensor_copy", out=smask_b[:], in_=smask_f[:])
        V("tensor_copy", out=snew_b[:], in_=snew_f[:])

        WBIG = sb("WBIG", [128, 40960], BF16)
        FA = sb("FA", [128, 9728])
        STGs = [sb("STG0", [128, 1280]), sb("STG1", [128, 1280]), sb("STG2", [128, 1280])]
        stg_n = [0]
        xts = [sb("xt0", [128, D]), sb("xt1", [128, D])]
        junks = [sb("junk0", [128, D]), sb("junk1", [128, D])]
        gbc = sb("gbc", [128, D])
        hbs = [sb("hb0", [128, D], BF16), sb("hb1", [128, D], BF16)]
        hTs = [sb("hT0", [128, 8, 128], BF16), sb("hT1", [128, 8, 128], BF16)]
        sss = [sb("ss0", [128, 4]), sb("ss1", [128, 4])]
        projs = [FA[:, 0:INW], FA[:, INW:2 * INW]]
        proj = projs[0]
        css = [sb("cs0", [128, 16]), sb("cs1", [128, 16])]
        utbs = [sb("utb0", [128, 4, 128]), sb("utb1", [128, 4, 128])]
        rt0 = sb("rt0", [128, 4, 24, 8])
        rts = [rt0, rt0]
        sm1s = [sb("sm1a", [128, 1024], BF16), sb("sm1b", [128, 1024], BF16)]
        sm2s = [sb("sm2a", [128, 1024], BF16), sb("sm2b", [128, 1024], BF16)]
        KT = [sb("KT%d" % i, [64, 4, 128], BF16) for i in range(3)]
        KTxs = [sb("KTx0", [64, 4, 128], BF16), sb("KTx1", [64, 4, 128], BF16)]
        VExs = [sb("VEx0", [128, 4, 65], BF16), sb("VEx1", [128, 4, 65], BF16)]
        QTs = [sb("QT0", [64, 4, 128], BF16), sb("QT1", [64, 4, 128], BF16)]
        VE = [sb("VE%d" % i, [128, 4, 65], BF16) for i in range(3)]
        QTS = sb("QTS", [64, 12, 32], BF16)
        KTS = sb("KTS", [64, 12, 32], BF16)
        VES = sb("VES", [8, 12, 65], BF16)
        accs = [sb("acc0", [128, 4, 65]), sb("acc1", [128, 4, 65])]
        acc2s = [sb("acc2a", [128, 4, 65]), sb("acc2b", [128, 4, 65])]
        acc3s = [sb("acc3a", [128, 4, 65]), sb("acc3b", [128, 4, 65])]
        rds = [sb("rd0", [128, 4, 1]), sb("rd1", [128, 4, 1])]
        pscale = sb("pscale", [128, 4])
        fix16 = sb("fix16", [128, 16])
        sss_big = [sb("spt0", [16, 512]), sb("spt1", [16, 512])]
        oaTs = [KT[0], KT[1]]
        ogs = [FA[:, 2560:2820], FA[:, 2820:3080]]
        qkvs = [FA[:, 0:768], FA[:, 768:1536]]
        cts = [FA[:, 1536:2048], FA[:, 2048:2560]]

        for i in range(3):
            V("memset", VE[i][:], 1.0)
            V("memset", KT[i][:], 0.0)
        for i in range(2):
            V("memset", VExs[i][:], 0.0)
            V("memset", KTxs[i][:], 0.0)
        V("memset", VES[:], 1.0)
        for i in range(2):
            V("tensor_copy", out=VExs[i][:, :, 64:65], in_=flag[:, 0:1].unsqueeze(1).to_broadcast([128, 4, 1]))

        def load_weight(dst_view, src_ap, rows, cols):
            c0 = 0
            while c0 < cols:
                n = min(1280, cols - c0)
                stg_n[0] += 1
                STG = STGs[stg_n[0] % 3]
                em.dma(STG[:rows, 0:n], src_ap[:, c0:c0 + n])
                ce = stg_n[0] % 3
                if ce == 0:
                    G("tensor_copy", out=dst_view[:, c0:c0 + n], in_=STG[:rows, 0:n])
                elif ce == 1:
                    V("tensor_copy", out=dst_view[:, c0:c0 + n], in_=STG[:rows, 0:n])
                else:
                    A("copy", out=dst_view[:, c0:c0 + n], in_=STG[:rows, 0:n])
                c0 += n

        def rmsnorm_rows(xt, junk, ss, rows):
            V("scalar_tensor_tensor", out=junk[:rows], in0=xt[:rows], scalar=1.0, in1=xt[:rows],
              op0=ALU.mult, op1=ALU.mult, accum_out=ss[:rows, 0:1])
            V("tensor_scalar", out=ss[:rows, 1:2], in0=ss[:rows, 0:1], scalar1=1.0 / D, scalar2=1e-6,
              op0=ALU.mult, op1=ALU.add)
            A("activation", out=ss[:rows, 2:3], in_=ss[:rows, 1:2], func=AF.Sqrt)
            V("reciprocal", out=ss[:rows, 3:4], in_=ss[:rows, 2:3])

        def transposes_bf(src, rows, nchunk, width, dst, coff=0):
            psT = next_psT()
            pv = psT[:, 0:nchunk * 128].rearrange("p (c t) -> p c t", c=nchunk)
            for k in range(nchunk):
                T("transpose", pv[:width, k, :rows], src[:rows, k * width:(k + 1) * width], ident_b[:rows, :rows])
            V("tensor_copy", out=dst[:width, coff:coff + nchunk, :rows], in_=pv[:width, :, :rows])

        for l in range(DEPTH):
            xsrc = xin if l == 0 else X
            for g, (W, d) in enumerate(GROUPS):
                for s_ in range(4):
                    em.dma(kvs[g][l, s_, 0:W - 8, :], caches[g][l, s_, 8:W, :], queue="scalar")
            WIN = WBIG[:, 0:8 * INW].rearrange("p (k c) -> p k c", k=8)
            for k in range(8):
                load_weight(WIN[:, k, :], w_in[l, k * 128:(k + 1) * 128, :], 128, INW)
            em.dma(gbc[:], norm1[l:l + 1, :].partition_broadcast(128))
            def a_bufs(tb):
                pr = tb & 1
                rows = 128 if tb < NPB else NSMP
                return rows, tb * 128, xts[pr], junks[pr], sss[pr], hbs[pr], hTs[pr], css[pr], utbs[pr], rts[pr], projs[pr]

            def a_pre(tb):
                rows, r0, xt, junk, ss, hb, hT, cs, utb, rt, pj = a_bufs(tb)
                em.dma(xt[:rows], xsrc[r0:r0 + rows, :])
                em.dma(cs[:rows], cstab[r0:r0 + rows, :])
                rmsnorm_rows(xt, junk, ss, rows)
                V("scalar_tensor_tensor", out=hb[:rows], in0=xt[:rows], scalar=ss[:rows, 3:4],
                  in1=gbc[:rows], op0=ALU.mult, op1=ALU.mult)

            a_tp = {}

            def a_T(tb):
                rows, r0, xt, junk, ss, hb, hT, cs, utb, rt, pj = a_bufs(tb)
                psT = next_psT()
                pv = psT[:, 0:1024].rearrange("p (c t) -> p c t", c=8)
                for k in range(8):
                    T("transpose", pv[:, k, :rows], hb[:rows, k * 128:(k + 1) * 128], ident_b[:rows, :rows])
                a_tp[tb] = pv

            def a_C(tb):
                rows, r0, xt, junk, ss, hb, hT, cs, utb, rt, pj = a_bufs(tb)
                V("tensor_copy", out=hT[:, 0:8, :rows], in_=a_tp.pop(tb)[:, :, :rows])

            def a_main(tb):
                rows, r0, xt, junk, ss, hb, hT, cs, utb, rt, pj = a_bufs(tb)
                for cg in range(10):
                    c0 = cg * 512
                    n = min(512, INW - c0)
                    psA = next_psA()
                    for k in range(8):
                        T("matmul", psA[:rows, 0:n], lhsT=hT[:, k, :rows], rhs=WIN[:, k, c0:c0 + n],
                          start=(k == 0), stop=(k == 7))
                    segs = []
                    if c0 + n <= 2816:
                        segs.append((c0, c0 + n, False))
                    elif c0 >= 2816:
                        segs.append((c0, c0 + n, True))
                    else:
                        segs.append((c0, 2816, False))
                        segs.append((2816, c0 + n, True))
                    for a, b, sig in segs:
                        if sig:
                            A("activation", out=pj[:rows, a:b], in_=psA[:rows, a - c0:b - c0], func=AF.Sigmoid)
                        else:
                            V("tensor_copy", out=pj[:rows, a:b], in_=psA[:rows, a - c0:b - c0])

            def a_U(tb):
                rows, r0, xt, junk, ss, hb, hT, cs, utb, rt, pj = a_bufs(tb)
                puv = psB[:, :].rearrange("p (c t) -> p c t", c=4)
                for c in range(4):
                    for k in range(8):
                        T("matmul", puv[:, c, :rows], lhsT=WIN[:, k, 2304 + 128 * c:2304 + 128 * (c + 1)],
                          rhs=hT[:, k, :rows], start=(k == 0), stop=(k == 7))

            def a_post(tb):
                rows, r0, xt, junk, ss, hb, hT, cs, utb, rt, pj = a_bufs(tb)
                puv = psB[:, :].rearrange("p (c t) -> p c t", c=4)
                A("copy", out=utb[:, :, :rows], in_=puv[:, :, :rows])
                em.dma(UT.rearrange("c f t -> f c t")[:, :, r0:r0 + rows], utb[:, :, :rows])
                pv = pj[:, 0:1536].rearrange("p (h e) -> p h e", h=24)
                x1 = pv[:rows, :, 0:8]
                x2 = pv[:rows, :, 8:16]
                cosb = cs[:rows, 0:8].unsqueeze(1).to_broadcast([rows, 24, 8])
                sinb = cs[:rows, 8:16].unsqueeze(1).to_broadcast([rows, 24, 8])
                V("tensor_tensor", out=rt[:rows, 0], in0=x1, in1=cosb, op=ALU.mult)
                V("tensor_tensor", out=rt[:rows, 1], in0=x2, in1=sinb, op=ALU.mult)
                V("tensor_tensor", out=rt[:rows, 2], in0=x1, in1=sinb, op=ALU.mult)
                V("tensor_tensor", out=rt[:rows, 3], in0=x2, in1=cosb, op=ALU.mult)
                V("tensor_tensor", out=x1, in0=rt[:rows, 0], in1=rt[:rows, 1], op=ALU.subtract)
                V("tensor_tensor", out=x2, in0=rt[:rows, 3], in1=rt[:rows, 2], op=ALU.add)
                em.dma(P[r0:r0 + rows, :], pj[:rows, :])

            a_pre(0)
            a_T(0)
            a_C(0)
            for tb in range(NBLK):
                if tb + 1 < NBLK:
                    a_pre(tb + 1)
                a_main(tb)
                if tb + 1 < NBLK:
                    a_T(tb + 1)
                a_U(tb)
                if tb + 1 < NBLK:
                    a_C(tb + 1)
                a_post(tb)

            xsec = {2: 0, 1: 2048, 0: 2560}
            for g, (W, d) in enumerate(GROUPS):
                em.dma(XB[xsec[g]:xsec[g] + W, 0:256], P[SEQ - W:SEQ, 768 + 256 * g:768 + 256 * g + 256])
                em.dma(XB[xsec[g]:xsec[g] + W, 256:512], P[SEQ - W:SEQ, 1536 + 256 * g:1536 + 256 * g + 256])
            em.dma(XB[2688:2704, :], P[SEQ - 16:SEQ, 2304:2816])
            for j in range(3):
                em.collective(XB[1024 * j:1024 * (j + 1), :], XGs[j][:, :])

            for g, (W, d) in enumerate(GROUPS):
                em.dma(kvp[g][l, :, 0:256], P[SEQ - W:SEQ, 768 + 256 * g:768 + 256 * g + 256])
                em.dma(kvp[g][l, :, 256:512], P[SEQ - W:SEQ, 1536 + 256 * g:1536 + 256 * g + 256])
                for s in range(4):
                    em.dma(kvs[g][l, s, W - 8:W, 0:256], P[SEQ + 8 * s:SEQ + 8 * s + 8, 768 + 256 * g:768 + 256 * g + 256])
                    em.dma(kvs[g][l, s, W - 8:W, 256:512], P[SEQ + 8 * s:SEQ + 8 * s + 8, 1536 + 256 * g:1536 + 256 * g + 256])
            em.dma(poolp[l, :, :], P[SEQ - 15:SEQ, 2304:2816])
            for s in range(4):
                em.dma(pools[l, s, 0:7, :], spool[l, s, 8:15, :])
                em.dma(pools[l, s, 7:15, :], P[SEQ + 8 * s:SEQ + 8 * s + 8, 2304:2816])

            psCs = [psC, psB]
            cblocks = []
            for g, (W, d) in enumerate(GROUPS):
                nbc = SEQ // d // 128
                for r in range(d):
                    for qb in range(nbc):
                        cblocks.append((g, d, r, qb))

            def c_pre(n):
                g, d, r, qb = cblocks[n]
                a0 = r + d * 128 * qb
                a1 = a0 + 127 * d + 1
                bp = n & 1
                cur = n % 3
                KTx, VEx = KTxs[bp], VExs[bp]
                qkv, qkb, QT = qkvs[bp], sm1s[bp][:, 0:512], QTs[bp]
                em.dma(qkv[:, 0:256], P[a0:a1:d, 256 * g:256 * g + 256])
                em.dma(qkv[:, 256:512], P[a0:a1:d, 768 + 256 * g:768 + 256 * g + 256])
                em.dma(qkv[:, 512:768], P[a0:a1:d, 1536 + 256 * g:1536 + 256 * g + 256])
                if qb == 0:
                    prevt = FA[:, 1536:2048] if (n & 1) == 0 else FA[:, 2048:2560]
                    pkb = sm1s[bp][:, 512:768]
                    if g == 2:
                        em.dma(prevt[0:64, :], XGs[0][r:r + 16 * 63 + 1:16, :])
                        em.dma(prevt[64:128, :], XGs[1][r:r + 16 * 63 + 1:16, :])
                    elif g == 1:
                        em.dma(prevt[:, :], XGs[2][r:r + 4 * 127 + 1:4, :])
                    else:
                        em.dma(prevt[:, :], XGs[2][512:640, :])
                    V("tensor_copy", out=pkb, in_=prevt[:, 0:256])
                    G("tensor_scalar", out=VEx[:, :, 0:64], in0=prevt[:, 256:512].rearrange("p (h e) -> p h e", h=4),
                      scalar1=flag[:, 0:1], scalar2=None, op0=ALU.mult)
                    psTp = next_psT()
                    tpv = psTp[:, 0:512].rearrange("p (c t) -> p c t", c=4)
                    for k in range(4):
                        T("transpose", tpv[0:64, k, :], pkb[:, 64 * k:64 * k + 64], ident_b[:, :])
                    A("copy", out=KTx[:, :, :], in_=tpv[0:64, :, :])
                    ktp = KTx
                else:
                    ktp = KT[(n - 1) % 3]
                V("tensor_copy", out=qkb, in_=qkv[:, 0:512])
                G("tensor_copy", out=VE[cur][:, :, 0:64], in_=qkv[:, 512:768].rearrange("p (h e) -> p h e", h=4))
                psT = next_psT()
                tqv = psT[:, 0:1024].rearrange("p (c t) -> p c t", c=8)
                for k in range(8):
                    T("transpose", tqv[0:64, k, :], qkb[:, 64 * k:64 * k + 64], ident_b[:, :])
                V("tensor_copy", out=QT[:, :, :], in_=tqv[0:64, 0:4, :])
                A("copy", out=KT[cur][:, :, :], in_=tqv[0:64, 4:8, :])
                for hp in range(2):
                    psD = psDs[hp]
                    pt = sm2s[bp][:, 512 * hp:512 * hp + 512]
                    for hh in range(2):
                        h = 2 * hp + hh
                        o = 256 * hh
                        T("matmul", psD[:, o:o + 128], lhsT=ktp[:, h, :], rhs=QT[:, h, :], start=True, stop=False)
                        T("matmul", psD[:, o:o + 128], lhsT=ident_b[:, :], rhs=mask_b[:, 0:128], start=False, stop=True)
                        T("matmul", psD[:, o + 128:o + 256], lhsT=KT[cur][:, h, :], rhs=QT[:, h, :], start=True, stop=False)
                        T("matmul", psD[:, o + 128:o + 256], lhsT=ident_b[:, :], rhs=mask_b[:, 128:256], start=False, stop=True)
                    A("activation", out=pt, in_=psD[:, 0:512], func=AF.Exp, scale=0.125)

            def c_post(n):
                g, d, r, qb = cblocks[n]
                a0 = r + d * 128 * qb
                a1 = a0 + 127 * d + 1
                bp = n & 1
                cur = n % 3
                og = ogs[bp]
                vep = VExs[bp] if qb == 0 else VE[(n - 1) % 3]
                pov = psCs[bp][:, 0:260].rearrange("p (h e) -> p h e", h=4)
                for hp in range(2):
                    pt = sm2s[bp][:, 512 * hp:512 * hp + 512]
                    for hh in range(2):
                        h = 2 * hp + hh
                        o = 256 * hh
                        T("matmul", pov[:, h, :], lhsT=pt[:, o:o + 128], rhs=vep[:, h, :], start=True, stop=False)
                        T("matmul", pov[:, h, :], lhsT=pt[:, o + 128:o + 256], rhs=VE[cur][:, h, :], start=False, stop=True)
                V("tensor_copy", out=og[:, :], in_=psCs[bp][:, 0:260])
                em.dma(OG[g, a0:a1:d, :], og[:, :])

            def stage_C():
                c_pre(0)
                for n in range(len(cblocks)):
                    if n + 1 < len(cblocks):
                        c_pre(n + 1)
                    c_post(n)

            qs = FA[0:32, 4096:5632]
            qsb = sm1s[0][0:32, 0:768]
            ksb = sm2s[0][0:32, 0:768]
            em.dma(qs, P[SEQ:NTOK, 0:1536])
            V("tensor_copy", out=qsb, in_=qs[:, 0:768])
            V("tensor_copy", out=ksb, in_=qs[:, 768:1536])
            psT = next_psT()
            t12 = psT[:, 0:384].rearrange("p (c t) -> p c t", c=12)
            for k in range(12):
                T("transpose", t12[0:64, k, :], qsb[:, 64 * k:64 * k + 64], ident_b[0:32, 0:32])
            V("tensor_copy", out=QTS[:, :, :], in_=t12[0:64, :, :])
            psT = next_psT()
            t12 = psT[:, 0:384].rearrange("p (c t) -> p c t", c=12)
            for k in range(12):
                T("transpose", t12[0:64, k, :], ksb[:, 64 * k:64 * k + 64], ident_b[0:32, 0:32])
            V("tensor_copy", out=KTS[:, :, :], in_=t12[0:64, :, :])
            vsn = FA[0:8, 5632:6400]
            ct4s = [FA[:, 0:2048].rearrange("p (b c) -> p b c", b=4), FA[:, 2048:4096].rearrange("p (b c) -> p b c", b=4)]
            kt4s = [WBIG[0:64, 0:2048].rearrange("p (c t) -> p c t", c=16),
                    WBIG[0:64, 2048:4096].rearrange("p (c t) -> p c t", c=16)]
            ve4s = [WBIG[:, 4096:5136].rearrange("p (b h e) -> p b h e", b=4, h=4),
                    WBIG[:, 5136:6176].rearrange("p (b h e) -> p b h e", b=4, h=4)]
            ckb4s = [sm1s[0][:, 0:1024].rearrange("p (b c) -> p b c", b=4), sm1s[1][:, 0:1024].rearrange("p (b c) -> p b c", b=4)]
            for i in range(2):
                V("memset", ve4s[i], 1.0)
            VESa = WBIG[0:8, 6176:6176 + 4 * 780].rearrange("p (s h e) -> p s h e", s=4, h=12)
            V("memset", VESa, 1.0)
            vsns = [FA[0:8, 5632:6400], FA[0:8, 6400:7168]]
            for s_ in range(4):
                em.dma(vsns[s_ & 1], P[SEQ + 8 * s_:SEQ + 8 * s_ + 8, 1536:2304])
                V("tensor_copy", out=VESa[:, s_, :, 0:64], in_=vsns[s_ & 1].rearrange("p (h e) -> p h e", h=12))
            items = []
            for s_ in range(4):
                first = True
                for g, (W, d) in enumerate(GROUPS):
                    nblk = W // 128
                    for b0 in range(0, nblk, 4):
                        items.append(("cache", s_, g, b0, min(4, nblk - b0), first))
                        first = False
                items.append(("new", s_, 0, 0, 0, False))

            def s_pre(n):
                kind, s_, g, b0, nb, first = items[n]
                bp = n & 1
                acc = accs[s_ & 1]
                psD = psDs[bp]
                if first:
                    V("memset", acc[0:8], 0.0)
                if kind == "cache":
                    ct, ckb, kt, ve, pts = ct4s[bp], ckb4s[bp], kt4s[bp], ve4s[bp], sm2s[bp][:, 0:128]
                    em.dma(ct[:, 0:nb, :], caches[g][l, s_, 128 * b0:128 * (b0 + nb), :].rearrange("(b p) c -> p b c", p=128))
                    V("tensor_copy", out=ckb[:, 0:nb, :], in_=ct[:, 0:nb, 0:256])
                    G("tensor_copy", out=ve[:, 0:nb, :, 0:64], in_=ct[:, 0:nb, 256:512].rearrange("p b (h e) -> p b h e", h=4))
                    for b in range(nb):
                        tkv = psTs[b // 2][:, 512 * (b % 2):512 * (b % 2) + 512].rearrange("p (c t) -> p c t", c=4)
                        for h in range(4):
                            T("transpose", tkv[0:64, h, :], ckb[:, b, 64 * h:64 * h + 64], ident_b[:, :])
                    for half in range((nb + 1) // 2):
                        nbh = min(2, nb - 2 * half)
                        src = psTs[half][:, 0:512 * nbh].rearrange("p (c t) -> p c t", c=4 * nbh)
                        if half == 0:
                            A("copy", out=kt[:, 0:4 * nbh, :], in_=src[0:64, :, :])
                        else:
                            V("tensor_copy", out=kt[:, 8:8 + 4 * nbh, :], in_=src[0:64, :, :])
                    pssv = psD[:, 0:128].rearrange("p (c q) -> p c q", c=16)
                    for b in range(nb):
                        for h in range(4):
                            T("matmul", pssv[:, 4 * b + h, :], lhsT=kt[:, 4 * b + h, :],
                              rhs=QTS[:, 4 * g + h, 8 * s_:8 * s_ + 8], start=True, stop=True)
                    A("activation", out=pts[:, 0:32 * nb], in_=psD[:, 0:32 * nb], func=AF.Exp, scale=0.125)
                    mv = 2 * g + (1 if b0 > 0 else 0)
                    V("tensor_tensor", out=pts[:, 0:32 * nb], in0=pts[:, 0:32 * nb], in1=smask_b[:, mv, 0:32 * nb], op=ALU.mult)
                else:
                    ptn = sm2s[bp][:, 128:224]
                    pssn = psD[:, 0:96].rearrange("p (c q) -> p c q", c=12)
                    for c12 in range(12):
                        T("matmul", pssn[0:8, c12, :], lhsT=KTS[:, c12, 8 * s_:8 * s_ + 8],
                          rhs=QTS[:, c12, 8 * s_:8 * s_ + 8], start=True, stop=True)
                    A("activation", out=ptn[0:8], in_=psD[0:8, 0:96], func=AF.Exp, scale=0.125)
                    V("tensor_tensor", out=ptn[0:8], in0=ptn[0:8], in1=snew_b[:, :], op=ALU.mult)

            def s_post(n):
                kind, s_, g, b0, nb, first = items[n]
                bp = n & 1
                acc = accs[s_ & 1]
                rd = rds[s_ & 1]
                psD = psDs[bp]
                posv = psD[:, 128:388].rearrange("p (h e) -> p h e", h=4)
                if kind == "cache":
                    ve, pts = ve4s[bp], sm2s[bp][:, 0:128]
                    for h in range(4):
                        for b in range(nb):
                            T("matmul", posv[0:8, h, :], lhsT=pts[:, 32 * b + 8 * h:32 * b + 8 * h + 8], rhs=ve[:, b, h, :],
                              start=(b == 0), stop=(b == nb - 1))
                    V("tensor_tensor", out=acc[0:8], in0=acc[0:8], in1=posv[0:8], op=ALU.add)
                else:
                    ptn = sm2s[bp][:, 128:224]
                    oas = junks[s_ & 1][0:8, 0:256]
                    for h in range(4):
                        for g3 in range(3):
                            T("matmul", posv[0:8, h, :], lhsT=ptn[0:8, 8 * (4 * g3 + h):8 * (4 * g3 + h) + 8],
                              rhs=VESa[0:8, s_, 4 * g3 + h, :], start=(g3 == 0), stop=(g3 == 2))
                    V("tensor_tensor", out=acc[0:8], in0=acc[0:8], in1=posv[0:8], op=ALU.add)
                    V("reciprocal", out=rd[0:8], in_=acc[0:8, :, 64:65])
                    V("tensor_tensor", out=oas.rearrange("p (h e) -> p h e", h=4), in0=acc[0:8, :, 0:64],
                      in1=rd[0:8].to_broadcast([8, 4, 64]), op=ALU.mult)
                    em.dma(OA[8 * s_:8 * s_ + 8, :], oas)

            s_pre(0)
            for n in range(len(items)):
                if n + 1 < len(items):
                    s_pre(n + 1)
                s_post(n)

            LIN = WBIG[:, 0:512].rearrange("p (c d) -> p c d", c=4)
            for c in range(4):
                load_weight(LIN[:, c, :], plin[l, c, :, :], 128, 128)
            em.dma(pscale[:], pscaleT[l, :, :])
            HS = SEQ // 2
            NJ = HS + 16
            sptx = junks[0][0:16, 0:512]
            em.dma(sptx, XGs[2][640:656, :])
            ue = FA[:, 0:NJ]
            sA = FA[:, NJ:2 * NJ]
            sB = FA[:, 2 * NJ:3 * NJ]
            pbts = [WBIG[:, 2048:2048 + HS], WBIG[:, 2048 + HS:2048 + 2 * HS]]
            obts = [WBIG[:, 2048 + 2 * HS:2048 + 3 * HS], WBIG[:, 2048 + 3 * HS:2048 + 4 * HS]]
            for c, w in enumerate(POOLW):
                for hh in range(2):
                    pbt, obt = pbts[hh], obts[hh]
                    t0 = HS * hh
                    if hh == 0:
                        T("transpose", psC[:, 64:80], sptx[0:16, 128 * c:128 * c + 128], ident_f[0:16, 0:16])
                        V("tensor_scalar", out=ue[:, 0:16], in0=psC[:, 64:80], scalar1=flag[:, 0:1], scalar2=None,
                          op0=ALU.mult)
                    else:
                        em.dma(ue[:, 0:16], UT[c, :, t0 - 16:t0])
                    em.dma(ue[:, 16:NJ], UT[c, :, t0:t0 + HS])
                    cur = ue
                    step = 1
                    bufs = [sA, sB]
                    bi = 0
                    while step < w:
                        lo = 2 * step - 1
                        nxt = bufs[bi]
                        bi ^= 1
                        V("tensor_tensor", out=nxt[:, lo:NJ], in0=cur[:, lo:NJ], in1=cur[:, lo - step:NJ - step], op=ALU.add)
                        cur = nxt
                        step *= 2
                    V("scalar_tensor_tensor", out=pbt, in0=cur[:, 16:NJ], scalar=1.0 / w,
                      in1=ue[:, 16:NJ], op0=ALU.mult, op1=ALU.subtract)
                    if hh == 0:
                        V("tensor_tensor", out=fix16[:, :], in0=cur[:, 16:32], in1=rcfix[:, c, :], op=ALU.mult)
                        V("tensor_tensor", out=pbt[:, 0:16], in0=fix16[:, :], in1=ue[:, 16:32], op=ALU.subtract)
                    for tt in range(HS // 512):
                        psA = next_psA()
                        T("matmul", psA[:, 0:512], lhsT=LIN[:, c, :], rhs=pbt[:, 512 * tt:512 * tt + 512],
                          start=True, stop=True)
                        V("tensor_scalar", out=obt[:, 512 * tt:512 * tt + 512], in0=psA[:, 0:512],
                          scalar1=pscale[:, c:c + 1], scalar2=None, op0=ALU.mult)
                    em.dma(OBT[c, :, t0:t0 + HS], obt)
                xs = xts[c & 1]
                ues = xs[:, 0:96].rearrange("p (s j) -> p s j", s=4)
                sAs = xs[:, 96:192].rearrange("p (s j) -> p s j", s=4)
                sBs = xs[:, 192:288].rearrange("p (s j) -> p s j", s=4)
                V("memset", ues[:, :, 0:1], 0.0)
                for s in range(4):
                    spt = sss_big[s & 1][0:15, 0:512]
                    em.dma(spt, spool[l, s, :, :])
                    T("transpose", psC[:, 16 * s:16 * s + 15], spt[0:15, 128 * c:128 * c + 128], ident_f[0:15, 0:15])
                    V("tensor_copy", out=ues[:, s, 1:16], in_=psC[:, 16 * s:16 * s + 15])
                em.dma(ues[:, :, 16:24], UT[c, :, SEQ:NTOK].rearrange("f (s t) -> f s t", s=4))
                cur = ues
                step = 1
                bufs = [sAs, sBs]
                bi = 0
                while step < w:
                    lo = 2 * step - 1
                    nxt = bufs[bi]
                    bi ^= 1
                    V("tensor_tensor", out=nxt[:, :, lo:24], in0=cur[:, :, lo:24], in1=cur[:, :, lo - step:24 - step], op=ALU.add)
                    cur = nxt
                    step *= 2
                sm1, sm2 = sm1s[c & 1], sm2s[c & 1]
                pbs = sm1[:, 0:32].rearrange("p (s t) -> p s t", s=4)
                V("scalar_tensor_tensor", out=pbs, in0=cur[:, :, 16:24], scalar=1.0 / w,
                  in1=ues[:, :, 16:24], op0=ALU.mult, op1=ALU.subtract)
                psA = next_psA()
                T("matmul", psA[:, 0:32], lhsT=LIN[:, c, :], rhs=sm1[:, 0:32], start=True, stop=True)
                V("tensor_scalar", out=sm2[:, 0:32], in0=psA[:, 0:32], scalar1=pscale[:, c:c + 1],
                  scalar2=None, op0=ALU.mult)
                em.dma(OBT[c, :, SEQ:NTOK], sm2[:, 0:32])

            stage_C()

            WPA = WBIG[0:64, 0:4096].rearrange("p (h c) -> p h c", h=4)
            WPB = WBIG[:, 4096:8192].rearrange("p (k c) -> p k c", k=4)
            WO = WBIG[:, 8192:16384].rearrange("p (k c) -> p k c", k=8)
            for h in range(4):
                load_weight(WPA[:, h, :], w_pa[l, 64 * h:64 * h + 64, :], 64, D)
            for k in range(4):
                load_weight(WPB[:, k, :], w_pb[l, 128 * k:128 * k + 128, :], 128, D)
            for k in range(8):
                load_weight(WO[:, k, :], w_o[l, 128 * k:128 * k + 128, :], 128, D)
            mTs = [WBIG[:, 16384:17408].rearrange("p (k t) -> p k t", k=8),
                   WBIG[:, 17408:18432].rearrange("p (k t) -> p k t", k=8)]
            def m_bufs(tb):
                pr = tb & 1
                rows = 128 if tb < NPB else NSMP
                return (rows, tb * 128, xts[pr], junks[pr], hTs[pr], accs[pr], acc2s[pr], acc3s[pr], rds[pr],
                        sm1s[pr][:, 0:256], sm2s[pr][:, 0:1024], oaTs[pr], mTs[pr],
                        projs[pr][:, 0:2048], projs[pr][:, 2048:3072], projs[pr][:, 3072:4096])

            def m_pre(tb):
                rows, r0, xt, junk, hT, acc, acc2, acc3, rd, oab, mxb, oaT, mT, gt, mx, tmpm = m_bufs(tb)
                if tb < NPB:
                    em.dma(acc[:].rearrange("p h e -> p (h e)"), OG[0, r0:r0 + 128, :])
                    em.dma(acc2[:].rearrange("p h e -> p (h e)"), OG[1, r0:r0 + 128, :])
                    em.dma(acc3[:].rearrange("p h e -> p (h e)"), OG[2, r0:r0 + 128, :])
                    G("tensor_tensor", out=acc[:], in0=acc[:], in1=acc2[:], op=ALU.add)
                    G("tensor_tensor", out=acc[:], in0=acc[:], in1=acc3[:], op=ALU.add)
                    V("reciprocal", out=rd[:], in_=acc[:, :, 64:65])
                    V("tensor_tensor", out=oab.rearrange("p (h e) -> p h e", h=4), in0=acc[:, :, 0:64],
                      in1=rd[:].to_broadcast([128, 4, 64]), op=ALU.mult)
                else:
                    em.dma(junk[0:32, 0:256], OA[:, :])
                    V("tensor_copy", out=oab[0:32], in_=junk[0:32, 0:256])
                em.dma(hT[:, 0:4, :rows], OBT.rearrange("c f t -> f c t")[:, :, r0:r0 + rows])
                em.dma(gt[:rows], P[r0:r0 + rows, 2816:INW])
                em.dma(xt[:rows], xsrc[r0:r0 + rows, :])
                transposes_bf(oab, rows, 4, 64, oaT)

            def m_mm1(tb):
                rows, r0, xt, junk, hT, acc, acc2, acc3, rd, oab, mxb, oaT, mT, gt, mx, tmpm = m_bufs(tb)
                for ng in range(2):
                    psA = next_psA()
                    for h in range(4):
                        T("matmul", psA[:rows, 0:512], lhsT=oaT[:, h, :rows], rhs=WPA[:, h, 512 * ng:512 * ng + 512],
                          start=(h == 0), stop=(h == 3))
                    V("tensor_tensor", out=mx[:rows, 512 * ng:512 * ng + 512], in0=psA[:rows, 0:512],
                      in1=gt[:rows, 512 * ng:512 * ng + 512], op=ALU.mult)
                    psA = next_psA()
                    for k in range(4):
                        T("matmul", psA[:rows, 0:512], lhsT=hT[:, k, :rows], rhs=WPB[:, k, 512 * ng:512 * ng + 512],
                          start=(k == 0), stop=(k == 3))
                    V("tensor_tensor", out=tmpm[:rows, 512 * ng:512 * ng + 512], in0=psA[:rows, 0:512],
                      in1=gt[:rows, 1024 + 512 * ng:1024 + 512 * ng + 512], op=ALU.mult)
                G("tensor_tensor", out=mxb[:rows], in0=mx[:rows], in1=tmpm[:rows], op=ALU.add)

            def m_T2(tb):
                rows, r0, xt, junk, hT, acc, acc2, acc3, rd, oab, mxb, oaT, mT, gt, mx, tmpm = m_bufs(tb)
                transposes_bf(mxb, rows, 8, 128, mT)

            def m_mm2(tb):
                rows, r0, xt, junk, hT, acc, acc2, acc3, rd, oab, mxb, oaT, mT, gt, mx, tmpm = m_bufs(tb)
                for ng in range(2):
                    psA = next_psA()
                    for k in range(8):
                        T("matmul", psA[:rows, 0:512], lhsT=mT[:, k, :rows], rhs=WO[:, k, 512 * ng:512 * ng + 512],
                          start=(k == 0), stop=(k == 7))
                    V("tensor_tensor", out=junk[:rows, 512 * ng:512 * ng + 512], in0=psA[:rows, 0:512],
                      in1=xt[:rows, 512 * ng:512 * ng + 512], op=ALU.add)
                em.dma(X[r0:r0 + rows, :], junk[:rows, :])

            m_pre(0)
            m_mm1(0)
            for tb in range(NBLK):
                if tb + 1 < NBLK:
                    m_pre(tb + 1)
                m_T2(tb)
                if tb + 1 < NBLK:
                    m_mm1(tb + 1)
                m_mm2(tb)

            em.dma(gbc[:], norm2[l:l + 1, :].partition_broadcast(128))
            WUP = WBIG[:, 0:16384].rearrange("p (k c) -> p k c", k=8)
            WDN = WBIG[:, 16384:32768].rearrange("p (k c) -> p k c", k=16)
            abs_ = [WBIG[:, 32768:34816], WBIG[:, 34816:36864]]
            aTs = [WBIG[:, 36864:38912].rearrange("p (k t) -> p k t", k=16),
                   WBIG[:, 38912:40960].rearrange("p (k t) -> p k t", k=16)]
            for hf in range(2):
                for k in range(8):
                    load_weight(WUP[:, k, :], w_up[l, 128 * k:128 * k + 128, 2048 * hf:2048 * hf + 2048], 128, 2048)
                for k in range(16):
                    load_weight(WDN[:, k, :], w_down[l, 2048 * hf + 128 * k:2048 * hf + 128 * k + 128, :], 128, D)
                def d_bufs(tb):
                    pr = tb & 1
                    rows = 128 if tb < NPB else NSMP
                    return (rows, tb * 128, xts[pr], junks[pr], sss[pr], hbs[pr], hTs[pr], abs_[pr], aTs[pr],
                            projs[0][:, 1024 * pr:1024 * pr + 1024], projs[1][:, 1024 * pr:1024 * pr + 1024])

                def d_pre(tb):
                    rows, r0, xt, junk, ss, hb, hT, ab, aT, x2t, outb = d_bufs(tb)
                    em.dma(xt[:rows], X[r0:r0 + rows, :])
                    if hf == 1:
                        em.dma(x2t[:rows], X2[r0:r0 + rows, :])
                    rmsnorm_rows(xt, junk, ss, rows)
                    V("scalar_tensor_tensor", out=hb[:rows], in0=xt[:rows], scalar=ss[:rows, 3:4],
                      in1=gbc[:rows], op0=ALU.mult, op1=ALU.mult)
                    transposes_bf(hb, rows, 8, 128, hT)

                def d_up(tb):
                    rows, r0, xt, junk, ss, hb, hT, ab, aT, x2t, outb = d_bufs(tb)
                    rls = [projs[0][:, 2048:2560], projs[0][:, 2560:3072]]
                    for ng in range(4):
                        psA = next_psA()
                        rl = rls[ng & 1]
                        for k in range(8):
                            T("matmul", psA[:rows, 0:512], lhsT=hT[:, k, :rows], rhs=WUP[:, k, 512 * ng:512 * ng + 512],
                              start=(k == 0), stop=(k == 7))
                        A("activation", out=rl[:rows], in_=psA[:rows, 0:512], func=AF.Relu)
                        G("tensor_tensor", out=ab[:rows, 512 * ng:512 * ng + 512], in0=rl[:rows], in1=rl[:rows], op=ALU.mult)

                def d_abT(tb):
                    rows, r0, xt, junk, ss, hb, hT, ab, aT, x2t, outb = d_bufs(tb)
                    for j in range(2):
                        psT = next_psT()
                        pv = psT[:, 0:1024].rearrange("p (c t) -> p c t", c=8)
                        for k in range(8):
                            T("transpose", pv[:, k, :rows], ab[:rows, (8 * j + k) * 128:(8 * j + k + 1) * 128],
                              ident_b[:rows, :rows])
                        if j == 0:
                            V("tensor_copy", out=aT[:, 8 * j:8 * j + 8, :rows], in_=pv[:, :, :rows])
                        else:
                            A("copy", out=aT[:, 8 * j:8 * j + 8, :rows], in_=pv[:, :, :rows])

                def d_down(tb):
                    rows, r0, xt, junk, ss, hb, hT, ab, aT, x2t, outb = d_bufs(tb)
                    res_src = xt if hf == 0 else x2t
                    for ng in range(2):
                        psA = next_psA()
                        for k in range(16):
                            T("matmul", psA[:rows, 0:512], lhsT=aT[:, k, :rows], rhs=WDN[:, k, 512 * ng:512 * ng + 512],
                              start=(k == 0), stop=(k == 15))
                        V("tensor_tensor", out=outb[:rows, 512 * ng:512 * ng + 512], in0=psA[:rows, 0:512],
                          in1=res_src[:rows, 512 * ng:512 * ng + 512], op=ALU.add)
                    em.dma((X2 if hf == 0 else X)[r0:r0 + rows, :], outb[:rows, :])

                d_pre(0)
                d_up(0)
                for tb in range(NBLK):
                    if tb + 1 < NBLK:
                        d_pre(tb + 1)
                    d_abT(tb)
                    if tb + 1 < NBLK:
                        d_up(tb + 1)
                    d_down(tb)

        em.dma(gbc[:], fnorm[0:1, :].partition_broadcast(128))
        for tb in range(NBLK):
            rows = 128 if tb < NPB else NSMP
            r0 = tb * 128
            pr = tb & 1
            xt, junk, ss = xts[pr], junks[pr], sss[pr]
            yo = proj[:, 1024 * pr:1024 * pr + 1024]
            em.dma(xt[:rows], X[r0:r0 + rows, :])
            rmsnorm_rows(xt, junk, ss, rows)
            V("scalar_tensor_tensor", out=yo[:rows], in0=xt[:rows], scalar=ss[:rows, 3:4],
              in1=gbc[:rows], op0=ALU.mult, op1=ALU.mult)
            em.dma(y[r0:r0 + rows, :], yo[:rows, :])
        em.flush()

        with nc.Block() as block:
            @block.sync
            def _(e):
                em.replay("sync", e)

            @block.tensor
            def _(e):
                em.replay("tensor", e)

            @block.vector
            def _(e):
                em.replay("vector", e)

            @block.scalar
            def _(e):
                em.replay("scalar", e)

            @block.gpsimd
            def _(e):
                em.replay("gpsimd", e)
    return nc


def _tables():
    pos = np.concatenate([np.arange(FULLSEQ), np.tile(8192 + np.arange(8), 4)]).astype(np.float32)
    inv = (500000.0 ** (-np.arange(0, 16, 2, dtype=np.float32) / 16)).astype(np.float32)
    ang = pos[:, None] * inv[None, :]
    cstab = np.concatenate([np.cos(ang), np.sin(ang)], axis=1).astype(np.float32)
    i = np.arange(128)[:, None]
    j = np.arange(128)[None, :]
    maskpc = np.where(np.concatenate([(i >= j), (i <= j)], axis=1), 0.0, -30000.0).astype(np.float32)
    smask = np.zeros((3, 2, 128, 32), np.float32)
    snew = np.zeros((3, 8, 32), np.float32)
    for g, (W, d) in enumerate(GROUPS):
        for v in range(2):
            for ii in range(128):
                for t in range(8):
                    ok = ((ii - t) % d == 0) and (v == 1 or ii >= t)
                    for h in range(4):
                        smask[g, v, ii, 8 * h + t] = 1.0 if ok else 0.0
        for tk in range(8):
            for t in range(8):
                ok = (tk <= t) and ((t - tk) % d == 0)
                for h in range(4):
                    snew[g, tk, 8 * h + t] = 1.0 if ok else 0.0
    rcfix = np.zeros((2, 4, 128, 16), np.float32)
    for c, w in enumerate(POOLW):
        for t in range(16):
            rcfix[0, c, :, t] = 1.0 / min(w, t + 1)
            rcfix[1, c, :, t] = 1.0 / w
    smask4 = np.zeros((3, 2, 128, 4, 32), np.float32)
    for g in range(3):
        for fb in range(2):
            for b in range(4):
                smask4[g, fb, :, b, :] = smask[g, 0 if (fb == 0 and b == 0) else 1]
    smask = smask4.reshape(3, 2, 128, 128)
    snew = np.ascontiguousarray(snew.transpose(1, 0, 2).reshape(8, 96))
    return cstab, maskpc, smask, snew, rcfix


def kernel(x_prompt, x_sample, cache_kv_w128, cache_kv_w512, cache_kv_w2048, state_pool,
           norm1, w_in, w_pa, w_pb, pool_lin, pool_scale, w_o, norm2, w_up, w_down, final_norm):
    f = lambda a: np.ascontiguousarray(np.asarray(a, dtype=np.float32))
    x_prompt, x_sample = f(x_prompt), f(x_sample)
    cch = [f(cache_kv_w128), f(cache_kv_w512), f(cache_kv_w2048)]
    state_pool = f(state_pool)
    cstab, maskpc, smask, snew, rcfix = _tables()
    shared = {
        "norm1": f(norm1), "norm2": f(norm2), "fnorm": f(final_norm).reshape(1, D),
        "w_in": f(w_in), "w_pa": f(w_pa), "w_pb": f(w_pb), "plin": f(pool_lin),
        "pscaleT": np.ascontiguousarray(f(pool_scale).reshape(DEPTH, 4, 128).transpose(0, 2, 1)),
        "w_o": f(w_o), "w_up": f(w_up), "w_down": f(w_down),
        "ident": np.eye(128, dtype=np.float32), "maskpc": maskpc,
        "smask": smask, "snew": snew,
    }
    in_maps = []
    for c in range(8):
        m = dict(shared)
        b, hf = c // 2, c % 2
        m["xin"] = np.ascontiguousarray(np.concatenate(
            [x_prompt[b, SEQ * hf:SEQ * hf + SEQ], x_sample[4 * c:4 * c + 4].reshape(NSMP, D)], axis=0))
        m["cstab"] = np.ascontiguousarray(np.concatenate([cstab[SEQ * hf:SEQ * hf + SEQ], cstab[FULLSEQ:]], axis=0))
        m["rcfix"] = np.ascontiguousarray(rcfix[hf])
        m["flag"] = np.full((128, 1), float(hf), np.float32)
        for g in range(3):
            W = GROUPS[g][0]
            m["cache%d" % g] = np.ascontiguousarray(cch[g][:, 4 * c:4 * c + 4].reshape(DEPTH, 4, W, 512))
        m["spool"] = np.ascontiguousarray(state_pool[:, 4 * c:4 * c + 4])
        in_maps.append(m)
    nc = build_program()
    res = run_bass_kernel_spmd(nc, in_maps, core_ids=list(range(8)))
    R = res.results
    y_prompt = np.stack([np.concatenate([R[2 * b]["y"][:SEQ], R[2 * b + 1]["y"][:SEQ]], axis=0) for b in range(4)], axis=0)
    y_sample = np.concatenate([R[c]["y"][SEQ:].reshape(4, 8, D) for c in range(8)], axis=0)
    outs = [y_prompt.astype(np.float32), y_sample.astype(np.float32)]
    for g in range(3):
        W = GROUPS[g][0]
        outs.append(np.stack([R[2 * b + 1]["kvp%d" % g].reshape(DEPTH, W, 2, 4, 64) for b in range(4)], axis=1))
    outs.append(np.stack([R[2 * b + 1]["poolp"] for b in range(4)], axis=1))
    for g in range(3):
        W = GROUPS[g][0]
        outs.append(np.concatenate([R[c]["kvs%d" % g].reshape(DEPTH, 4, W, 2, 4, 64) for c in range(8)], axis=1))
    outs.append(np.concatenate([R[c]["pools"] for c in range(8)], axis=1))
    return tuple(np.ascontiguousarray(o, dtype=np.float32) for o in outs)
```

```python
from contextlib import ExitStack
import numpy as np
import concourse.bass as bass
import concourse.mybir as mybir
from concourse.bass_utils import run_bass_kernel_spmd

F32 = mybir.dt.float32
BF16 = mybir.dt.bfloat16
ALU = mybir.AluOpType
AF = mybir.ActivationFunctionType

D = 1024
FULLSEQ = 4096
SEQ = 2048
NSMP = 32
NTOK = SEQ + NSMP
NBLK = 17
NPB = 16
PAIRS = [[0, 1], [2, 3], [4, 5], [6, 7]]
DEPTH = 4
INW = 4864
GROUPS = ((128, 1), (512, 4), (2048, 16))
POOLW = (2, 4, 8, 16)
SEM_LIMIT = 30000
NSLOT = 32
SAME_ENGINE_WAIT = True
ENGS = ("sync", "tensor", "vector", "scalar", "gpsimd")


def _is_ap(x):
    return hasattr(x, "ap") and hasattr(x, "offset") and hasattr(x, "space")


def _is_dram(ap):
    return "DRAM" in str(ap.space).upper()


def _region(ap):
    isz = 4 if ap.dtype == F32 else 2
    dims = ap.ap
    off = ap.offset
    if _is_dram(ap):
        ext = sum(abs(s) * (c - 1) for s, c in dims)
        return (ap.name, 0, 1, off * isz, (off + ext + 1) * isz)
    if "PSUM" in str(ap.space).upper():
        return (ap.name, 0, 128, 0, 1 << 30)
    pstep, pcount = dims[0]
    p0 = off // pstep if pstep > 0 else 0
    f0 = off - p0 * pstep
    ext = sum(abs(s) * (c - 1) for s, c in dims[1:])
    return (ap.name, p0, p0 + pcount, f0 * isz, (f0 + ext + 1) * isz)


def _ovl(a, b):
    return a[1] < b[2] and b[1] < a[2] and a[3] < b[4] and b[3] < a[4]


def _contains(a, b):
    return a[1] <= b[1] and b[2] <= a[2] and a[3] <= b[3] and b[4] <= a[4]


class Em:
    def __init__(self, sems):
        self.sems = sems
        self.next_sem = 0
        self.cur = {}
        self.streams = {k: [] for k in ENGS}
        self.recs = {}
        self.known = {k: {} for k in ENGS}
        self.slots = {"sync": [None] * NSLOT, "scalar": [None] * 8}
        self.slot_n = {"sync": 0, "scalar": 0}
        self.pending = []
        self.seen_load = False
        self.nops = 0

    def _new_sem(self):
        i = self.next_sem
        self.next_sem += 1
        assert self.next_sem <= len(self.sems), "out of semaphores"
        return i

    def _alloc(self, eng):
        if eng not in self.cur or self.cur[eng][1] + 1 > SEM_LIMIT:
            self.cur[eng] = [self._new_sem(), 0]
        st = self.cur[eng]
        st[1] += 1
        return st[0], st[1]

    def _deps(self, eng, reads, writes, ref):
        deps = []
        for r in reads:
            lst = self.recs.setdefault(r[0], [])
            for rec in lst:
                if rec[1] and _ovl(rec[0], r):
                    deps.append(rec[2])
            lst[:] = [rec for rec in lst if not ((not rec[1]) and rec[2][3] == eng and rec[2][0] == ref[0]
                                                 and _contains(r, rec[0]))]
            lst.append((r, False, ref))
        for w in writes:
            lst = self.recs.setdefault(w[0], [])
            keep = []
            for rec in lst:
                if _ovl(rec[0], w):
                    if rec[2] is not ref:
                        deps.append(rec[2])
                    if _contains(w, rec[0]):
                        continue
                keep.append(rec)
            keep.append((w, True, ref))
            self.recs[w[0]] = keep
        return deps

    def _waits(self, eng, deps, is_dma):
        waits = []
        kn = self.known[eng]
        for kind, si, val, deng in deps:
            if kind == "c" and deng == eng and not is_dma and (eng == "tensor" or not SAME_ENGINE_WAIT):
                continue
            if kn.get(si, 0) >= val:
                continue
            kn[si] = val
            waits.append((si, val))
        return waits

    def flush(self):
        for deps, out, in_, si, ref in self.pending:
            waits = self._waits("sync", deps, True)
            self.streams["sync"].append((waits, "dma_start", (), {"out": out, "in_": in_}, si, 16))
        self.pending = []
        self.seen_load = False

    def op(self, eng, meth, a, k):
        if self.pending and self.seen_load:
            self.flush()
        outs = []
        if "out" in k:
            outs.append(k["out"])
        else:
            outs.append(a[0])
        if k.get("accum_out", None) is not None:
            outs.append(k["accum_out"])
        ins = [x for x in list(a[(0 if "out" in k else 1):]) if _is_ap(x)]
        ins += [v for kk, v in k.items() if kk not in ("out", "accum_out") and _is_ap(v)]
        si, val = self._alloc(eng)
        ref = ("c", si, val, eng)
        psum_ins = [x for x in ins if "PSUM" in str(x.space).upper()]
        ins = [x for x in ins if "PSUM" not in str(x.space).upper()]
        deps = self._deps(eng, [_region(x) for x in ins], [_region(x) for x in outs + psum_ins], ref)
        if self.pending:
            prefs = [p[4] for p in self.pending]
            if any(any(d is p for p in prefs) for d in deps):
                self.flush()
        waits = self._waits(eng, deps, False)
        self.streams[eng].append((waits, meth, a, k, si, 1))
        self.nops += 1

    def dma(self, out, in_, queue="sync"):
        is_store = _is_dram(out) and not _is_dram(in_) and queue == "sync"
        nsl = len(self.slots[queue])
        sl = self.slot_n[queue] % nsl
        self.slot_n[queue] += 1
        prev = self.slots[queue][sl]
        if prev is None:
            si, val = self._new_sem(), 16
        else:
            si, val = prev[1], prev[2] + 16
        ref = ("d", si, val, queue)
        deps = self._deps(queue, [_region(in_)], [_region(out)], ref)
        if prev is not None:
            deps.append(prev)
        self.slots[queue][sl] = ref
        self.nops += 1
        if queue != "sync":
            if self.pending:
                prefs = [p[4] for p in self.pending]
                if any(any(d is p for p in prefs) for d in deps):
                    self.flush()
            waits = self._waits(queue, deps, True)
            self.streams[queue].append((waits, "dma_start", (), {"out": out, "in_": in_}, si, 16))
            return
        if is_store:
            self.pending.append((deps, out, in_, si, ref))
            return
        if self.pending:
            prefs = [p[4] for p in self.pending]
            if any(any(d is p for p in prefs) for d in deps):
                self.flush()
        waits = self._waits("sync", deps, True)
        self.streams["sync"].append((waits, "dma_start", (), {"out": out, "in_": in_}, si, 16))
        self.seen_load = True

    def collective(self, in_ap, out_ap):
        if self.pending:
            self.flush()
        si = self._new_sem()
        ref = ("d", si, 1, "gpsimd")
        deps = self._deps("gpsimd", [_region(in_ap)], [_region(out_ap)], ref)
        waits = self._waits("gpsimd", deps, True)
        self.streams["gpsimd"].append((waits, "collective_compute", ("AllGather", ALU.bypass),
                                       dict(replica_groups=PAIRS, ins=[in_ap.opt()], outs=[out_ap.opt()]), si, None))
        self.nops += 1

    def replay(self, eng, handle):
        for waits, meth, a, k, si, inc in self.streams[eng]:
            for wsi, wval in waits:
                handle.wait_ge(self.sems[wsi], wval)
            inst = getattr(handle, meth)(*a, **k)
            if inc is None:
                inst.then_inc(self.sems[si])
            else:
                inst.then_inc(self.sems[si], inc)
        if eng == "sync":
            for q in self.slots:
                for ref in self.slots[q]:
                    if ref is not None:
                        handle.wait_ge(self.sems[ref[1]], ref[2])


def build_program():
    nc = bass.Bass("TRN2", target_bir_lowering=False)

    def din(name, shape, dt=F32):
        return nc.dram_tensor(name, list(shape), dt, kind="ExternalInput").ap()

    def dout(name, shape):
        return nc.dram_tensor(name, list(shape), F32, kind="ExternalOutput").ap()

    def dscr(name, shape, dt=F32):
        return nc.dram_tensor(name, list(shape), dt, kind="Internal").ap()

    xin = din("xin", [NTOK, D])
    caches = [din("cache%d" % g, [DEPTH, 4, GROUPS[g][0], 512]) for g in range(3)]
    spool = din("spool", [DEPTH, 4, 15, 512])
    norm1 = din("norm1", [DEPTH, D])
    norm2 = din("norm2", [DEPTH, D])
    fnorm = din("fnorm", [1, D])
    w_in = din("w_in", [DEPTH, D, INW])
    w_pa = din("w_pa", [DEPTH, 256, D])
    w_pb = din("w_pb", [DEPTH, 512, D])
    plin = din("plin", [DEPTH, 4, 128, 128])
    pscaleT = din("pscaleT", [DEPTH, 128, 4])
    w_o = din("w_o", [DEPTH, D, D])
    w_up = din("w_up", [DEPTH, D, 4 * D])
    w_down = din("w_down", [DEPTH, 4 * D, D])
    cstab = din("cstab", [NTOK, 16])
    ident_in = din("ident", [128, 128])
    mask_in = din("maskpc", [128, 256])
    smask_in = din("smask", [3, 2, 128, 128])
    snew_in = din("snew", [8, 96])
    rcfix_in = din("rcfix", [4, 128, 16])
    flag_in = din("flag", [128, 1])

    y = dout("y", [NTOK, D])
    kvp = [dout("kvp%d" % g, [DEPTH, GROUPS[g][0], 512]) for g in range(3)]
    poolp = dout("poolp", [DEPTH, 15, 512])
    kvs = [dout("kvs%d" % g, [DEPTH, 4, GROUPS[g][0], 512]) for g in range(3)]
    pools = dout("pools", [DEPTH, 4, 15, 512])

    X = dscr("X", [NTOK, D])
    X2 = dscr("X2", [NTOK, D])
    P = dscr("P", [NTOK, INW])
    UT = dscr("UT", [4, 128, NTOK])
    OBT = dscr("OBT", [4, 128, NTOK], BF16)
    OG = dscr("OG", [3, SEQ, 260])
    OA = dscr("OA", [NSMP, 256])
    XB = dscr("XB", [3072, 512])
    XGs = [dscr("XG%d" % j, [2048, 512]) for j in range(3)]

    with ExitStack() as es:
        sems = [es.enter_context(nc.semaphore("s%d" % i)) for i in range(90)]
        em = Em(sems)

        def sb(name, shape, dt=F32):
            return es.enter_context(nc.sbuf_tensor("sb_" + name, list(shape), dt))

        def ps(name, shape, dt=F32):
            return es.enter_context(nc.psum_tensor("ps_" + name, list(shape), dt))

        psAs = [ps("psA0", [128, 512]), ps("psA1", [128, 512])]
        psB = ps("psB", [128, 512])
        psC = ps("psC", [128, 512])
        psDs = [ps("psD0", [128, 512]), ps("psD1", [128, 512])]
        psTs = [ps("psT0", [128, 1024], BF16), ps("psT1", [128, 1024], BF16)]
        pa_n = [0]
        pt_n = [0]

        def next_psA():
            pa_n[0] += 1
            return psAs[pa_n[0] & 1]

        def next_psT():
            pt_n[0] += 1
            return psTs[pt_n[0] & 1]

        ident_f = sb("ident_f", [128, 128])
        ident_b = sb("ident_b", [128, 128], BF16)
        mask_f = sb("mask_f", [128, 256])
        mask_b = sb("mask_b", [128, 256], BF16)
        smask_f = sb("smask_f", [128, 6, 128])
        smask_b = sb("smask_b", [128, 6, 128], BF16)
        snew_f = sb("snew_f", [8, 96])
        snew_b = sb("snew_b", [8, 96], BF16)
        rcfix = sb("rcfix", [128, 4, 16])

        V = lambda m, *a, **k: em.op("vector", m, a, k)
        A = lambda m, *a, **k: em.op("scalar", m, a, k)
        G = lambda m, *a, **k: em.op("gpsimd", m, a, k)
        T = lambda m, *a, **k: em.op("tensor", m, a, k)

        em.dma(ident_f[:], ident_in[:, :])
        em.dma(mask_f[:], mask_in[:, :])
        em.dma(smask_f[:], smask_in.rearrange("g v p q -> p (g v) q"))
        em.dma(snew_f[:], snew_in[:, :])
        em.dma(rcfix[:], rcfix_in.rearrange("c p t -> p c t"))
        flag = sb("flag", [128, 1])
        em.dma(flag[:], flag_in[:, :])
        V("tensor_copy", out=ident_b[:], in_=ident_f[:])
        V("tensor_copy", out=mask_b[:], in_=mask_f[:])
        V("tensor_copy", out=smask_b[:], in_=smask_f[:])
        V("tensor_copy", out=snew_b[:], in_=snew_f[:])

        WBIG = sb("WBIG", [128, 40960], BF16)
        FA = sb("FA", [128, 9728])
        STGs = [sb("STG0", [128, 1280]), sb("STG1", [128, 1280]), sb("STG2", [128, 1280])]
        stg_n = [0]
        xts = [sb("xt0", [128, D]), sb("xt1", [128, D])]
        junks = [sb("junk0", [128, D]), sb("junk1", [128, D])]
        gbc = sb("gbc", [128, D])
        hbs = [sb("hb0", [128, D], BF16), sb("hb1", [128, D], BF16)]
        hTs = [sb("hT0", [128, 8, 128], BF16), sb("hT1", [128, 8, 128], BF16)]
        sss = [sb("ss0", [128, 4]), sb("ss1", [128, 4])]
        projs = [FA[:, 0:INW], FA[:, INW:2 * INW]]
        proj = projs[0]
        css = [sb("cs0", [128, 16]), sb("cs1", [128, 16])]
        utbs = [sb("utb0", [128, 4, 128]), sb("utb1", [128, 4, 128])]
        rt0 = sb("rt0", [128, 4, 24, 8])
        rts = [rt0, rt0]
        sm1s = [sb("sm1a", [128, 1024], BF16), sb("sm1b", [128, 1024], BF16)]
        sm2s = [sb("sm2a", [128, 1024], BF16), sb("sm2b", [128, 1024], BF16)]
        KT = [sb("KT%d" % i, [64, 4, 128], BF16) for i in range(3)]
        KTxs = [sb("KTx0", [64, 4, 128], BF16), sb("KTx1", [64, 4, 128], BF16)]
        VExs = [sb("VEx0", [128, 4, 65], BF16), sb("VEx1", [128, 4, 65], BF16)]
        QTs = [sb("QT0", [64, 4, 128], BF16), sb("QT1", [64, 4, 128], BF16)]
        VE = [sb("VE%d" % i, [128, 4, 65], BF16) for i in range(3)]
        QTS = sb("QTS", [64, 12, 32], BF16)
        KTS = sb("KTS", [64, 12, 32], BF16)
        VES = sb("VES", [8, 12, 65], BF16)
        accs = [sb("acc0", [128, 4, 65]), sb("acc1", [128, 4, 65])]
        acc2s = [sb("acc2a", [128, 4, 65]), sb("acc2b", [128, 4, 65])]
        acc3s = [sb("acc3a", [128, 4, 65]), sb("acc3b", [128, 4, 65])]
        rds = [sb("rd0", [128, 4, 1]), sb("rd1", [128, 4, 1])]
        pscale = sb("pscale", [128, 4])
        fix16 = sb("fix16", [128, 16])
        sss_big = [sb("spt0", [16, 512]), sb("spt1", [16, 512])]
        oaTs = [KT[0], KT[1]]
        ogs = [FA[:, 2560:2820], FA[:, 2820:3080]]
        qkvs = [FA[:, 0:768], FA[:, 768:1536]]
        cts = [FA[:, 1536:2048], FA[:, 2048:2560]]

        for i in range(3):
            V("memset", VE[i][:], 1.0)
            V("memset", KT[i][:], 0.0)
        for i in range(2):
            V("memset", VExs[i][:], 0.0)
            V("memset", KTxs[i][:], 0.0)
        V("memset", VES[:], 1.0)
        for i in range(2):
            V("tensor_copy", out=VExs[i][:, :, 64:65], in_=flag[:, 0:1].unsqueeze(1).to_broadcast([128, 4, 1]))

        def load_weight(dst_view, src_ap, rows, cols):
            c0 = 0
            while c0 < cols:
                n = min(1280, cols - c0)
                stg_n[0] += 1
                STG = STGs[stg_n[0] % 3]
                em.dma(STG[:rows, 0:n], src_ap[:, c0:c0 + n])
                ce = stg_n[0] % 3
                if ce == 0:
                    G("tensor_copy", out=dst_view[:, c0:c0 + n], in_=STG[:rows, 0:n])
                elif ce == 1:
                    V("tensor_copy", out=dst_view[:, c0:c0 + n], in_=STG[:rows, 0:n])
                else:
                    A("copy", out=dst_view[:, c0:c0 + n], in_=STG[:rows, 0:n])
                c0 += n

        def rmsnorm_rows(xt, junk, ss, rows):
            V("scalar_tensor_tensor", out=junk[:rows], in0=xt[:rows], scalar=1.0, in1=xt[:rows],
              op0=ALU.mult, op1=ALU.mult, accum_out=ss[:rows, 0:1])
            V("tensor_scalar", out=ss[:rows, 1:2], in0=ss[:rows, 0:1], scalar1=1.0 / D, scalar2=1e-6,
              op0=ALU.mult, op1=ALU.add)
            A("activation", out=ss[:rows, 2:3], in_=ss[:rows, 1:2], func=AF.Sqrt)
            V("reciprocal", out=ss[:rows, 3:4], in_=ss[:rows, 2:3])

        def transposes_bf(src, rows, nchunk, width, dst, coff=0):
            psT = next_psT()
            pv = psT[:, 0:nchunk * 128].rearrange("p (c t) -> p c t", c=nchunk)
            for k in range(nchunk):
                T("transpose", pv[:width, k, :rows], src[:rows, k * width:(k + 1) * width], ident_b[:rows, :rows])
            V("tensor_copy", out=dst[:width, coff:coff + nchunk, :rows], in_=pv[:width, :, :rows])

        for l in range(DEPTH):
            xsrc = xin if l == 0 else X
            for g, (W, d) in enumerate(GROUPS):
                for s_ in range(4):
                    em.dma(kvs[g][l, s_, 0:W - 8, :], caches[g][l, s_, 8:W, :], queue="scalar")
            WIN = WBIG[:, 0:8 * INW].rearrange("p (k c) -> p k c", k=8)
            for k in range(8):
                load_weight(WIN[:, k, :], w_in[l, k * 128:(k + 1) * 128, :], 128, INW)
            em.dma(gbc[:], norm1[l:l + 1, :].partition_broadcast(128))
            def a_bufs(tb):
                pr = tb & 1
                rows = 128 if tb < NPB else NSMP
                return rows, tb * 128, xts[pr], junks[pr], sss[pr], hbs[pr], hTs[pr], css[pr], utbs[pr], rts[pr], projs[pr]

            def a_pre(tb):
                rows, r0, xt, junk, ss, hb, hT, cs, utb, rt, pj = a_bufs(tb)
                em.dma(xt[:rows], xsrc[r0:r0 + rows, :])
                em.dma(cs[:rows], cstab[r0:r0 + rows, :])
                rmsnorm_rows(xt, junk, ss, rows)
                V("scalar_tensor_tensor", out=hb[:rows], in0=xt[:rows], scalar=ss[:rows, 3:4],
                  in1=gbc[:rows], op0=ALU.mult, op1=ALU.mult)

            a_tp = {}

            def a_T(tb):
                rows, r0, xt, junk, ss, hb, hT, cs, utb, rt, pj = a_bufs(tb)
                psT = next_psT()
                pv = psT[:, 0:1024].rearrange("p (c t) -> p c t", c=8)
                for k in range(8):
                    T("transpose", pv[:, k, :rows], hb[:rows, k * 128:(k + 1) * 128], ident_b[:rows, :rows])
                a_tp[tb] = pv

            def a_C(tb):
                rows, r0, xt, junk, ss, hb, hT, cs, utb, rt, pj = a_bufs(tb)
                V("tensor_copy", out=hT[:, 0:8, :rows], in_=a_tp.pop(tb)[:, :, :rows])

            def a_main(tb):
                rows, r0, xt, junk, ss, hb, hT, cs, utb, rt, pj = a_bufs(tb)
                for cg in range(10):
                    c0 = cg * 512
                    n = min(512, INW - c0)
                    psA = next_psA()
                    for k in range(8):
                        T("matmul", psA[:rows, 0:n], lhsT=hT[:, k, :rows], rhs=WIN[:, k, c0:c0 + n],
                          start=(k == 0), stop=(k == 7))
                    segs = []
                    if c0 + n <= 2816:
                        segs.append((c0, c0 + n, False))
                    elif c0 >= 2816:
                        segs.append((c0, c0 + n, True))
                    else:
                        segs.append((c0, 2816, False))
                        segs.append((2816, c0 + n, True))
                    for a, b, sig in segs:
                        if sig:
                            A("activation", out=pj[:rows, a:b], in_=psA[:rows, a - c0:b - c0], func=AF.Sigmoid)
                        else:
                            V("tensor_copy", out=pj[:rows, a:b], in_=psA[:rows, a - c0:b - c0])

            def a_U(tb):
                rows, r0, xt, junk, ss, hb, hT, cs, utb, rt, pj = a_bufs(tb)
                puv = psB[:, :].rearrange("p (c t) -> p c t", c=4)
                for c in range(4):
                    for k in range(8):
                        T("matmul", puv[:, c, :rows], lhsT=WIN[:, k, 2304 + 128 * c:2304 + 128 * (c + 1)],
                          rhs=hT[:, k, :rows], start=(k == 0), stop=(k == 7))

            def a_post(tb):
                rows, r0, xt, junk, ss, hb, hT, cs, utb, rt, pj = a_bufs(tb)
                puv = psB[:, :].rearrange("p (c t) -> p c t", c=4)
                A("copy", out=utb[:, :, :rows], in_=puv[:, :, :rows])
                em.dma(UT.rearrange("c f t -> f c t")[:, :, r0:r0 + rows], utb[:, :, :rows])
                pv = pj[:, 0:1536].rearrange("p (h e) -> p h e", h=24)
                x1 = pv[:rows, :, 0:8]
                x2 = pv[:rows, :, 8:16]
                cosb = cs[:rows, 0:8].unsqueeze(1).to_broadcast([rows, 24, 8])
                sinb = cs[:rows, 8:16].unsqueeze(1).to_broadcast([rows, 24, 8])
                V("tensor_tensor", out=rt[:rows, 0], in0=x1, in1=cosb, op=ALU.mult)
                V("tensor_tensor", out=rt[:rows, 1], in0=x2, in1=sinb, op=ALU.mult)
                V("tensor_tensor", out=rt[:rows, 2], in0=x1, in1=sinb, op=ALU.mult)
                V("tensor_tensor", out=rt[:rows, 3], in0=x2, in1=cosb, op=ALU.mult)
                V("tensor_tensor", out=x1, in0=rt[:rows, 0], in1=rt[:rows, 1], op=ALU.subtract)
                V("tensor_tensor", out=x2, in0=rt[:rows, 3], in1=rt[:rows, 2], op=ALU.add)
                em.dma(P[r0:r0 + rows, :], pj[:rows, :])

            a_pre(0)
            a_T(0)
            a_C(0)
            for tb in range(NBLK):
                if tb + 1 < NBLK:
                    a_pre(tb + 1)
                a_main(tb)
                if tb + 1 < NBLK:
                    a_T(tb + 1)
                a_U(tb)
                if tb + 1 < NBLK:
                    a_C(tb + 1)
                a_post(tb)

            xsec = {2: 0, 1: 2048, 0: 2560}
            for g, (W, d) in enumerate(GROUPS):
                em.dma(XB[xsec[g]:xsec[g] + W, 0:256], P[SEQ - W:SEQ, 768 + 256 * g:768 + 256 * g + 256])
                em.dma(XB[xsec[g]:xsec[g] + W, 256:512], P[SEQ - W:SEQ, 1536 + 256 * g:1536 + 256 * g + 256])
            em.dma(XB[2688:2704, :], P[SEQ - 16:SEQ, 2304:2816])
            for j in range(3):
                em.collective(XB[1024 * j:1024 * (j + 1), :], XGs[j][:, :])

            for g, (W, d) in enumerate(GROUPS):
                em.dma(kvp[g][l, :, 0:256], P[SEQ - W:SEQ, 768 + 256 * g:768 + 256 * g + 256])
                em.dma(kvp[g][l, :, 256:512], P[SEQ - W:SEQ, 1536 + 256 * g:1536 + 256 * g + 256])
                for s in range(4):
                    em.dma(kvs[g][l, s, W - 8:W, 0:256], P[SEQ + 8 * s:SEQ + 8 * s + 8, 768 + 256 * g:768 + 256 * g + 256])
                    em.dma(kvs[g][l, s, W - 8:W, 256:512], P[SEQ + 8 * s:SEQ + 8 * s + 8, 1536 + 256 * g:1536 + 256 * g + 256])
            em.dma(poolp[l, :, :], P[SEQ - 15:SEQ, 2304:2816])
            for s in range(4):
                em.dma(pools[l, s, 0:7, :], spool[l, s, 8:15, :])
                em.dma(pools[l, s, 7:15, :], P[SEQ + 8 * s:SEQ + 8 * s + 8, 2304:2816])

            psCs = [psC, psB]
            cblocks = []
            for g, (W, d) in enumerate(GROUPS):
                nbc = SEQ // d // 128
                for r in range(d):
                    for qb in range(nbc):
                        cblocks.append((g, d, r, qb))

            def c_pre(n):
                g, d, r, qb = cblocks[n]
                a0 = r + d * 128 * qb
                a1 = a0 + 127 * d + 1
                bp = n & 1
                cur = n % 3
                KTx, VEx = KTxs[bp], VExs[bp]
                qkv, qkb, QT = qkvs[bp], sm1s[bp][:, 0:512], QTs[bp]
                em.dma(qkv[:, 0:256], P[a0:a1:d, 256 * g:256 * g + 256])
                em.dma(qkv[:, 256:512], P[a0:a1:d, 768 + 256 * g:768 + 256 * g + 256])
                em.dma(qkv[:, 512:768], P[a0:a1:d, 1536 + 256 * g:1536 + 256 * g + 256])
                if qb == 0:
                    prevt = FA[:, 1536:2048] if (n & 1) == 0 else FA[:, 2048:2560]
                    pkb = sm1s[bp][:, 512:768]
                    if g == 2:
                        em.dma(prevt[0:64, :], XGs[0][r:r + 16 * 63 + 1:16, :])
                        em.dma(prevt[64:128, :], XGs[1][r:r + 16 * 63 + 1:16, :])
                    elif g == 1:
                        em.dma(prevt[:, :], XGs[2][r:r + 4 * 127 + 1:4, :])
                    else:
                        em.dma(prevt[:, :], XGs[2][512:640, :])
                    V("tensor_copy", out=pkb, in_=prevt[:, 0:256])
                    G("tensor_scalar", out=VEx[:, :, 0:64], in0=prevt[:, 256:512].rearrange("p (h e) -> p h e", h=4),
                      scalar1=flag[:, 0:1], scalar2=None, op0=ALU.mult)
                    psTp = next_psT()
                    tpv = psTp[:, 0:512].rearrange("p (c t) -> p c t", c=4)
                    for k in range(4):
                        T("transpose", tpv[0:64, k, :], pkb[:, 64 * k:64 * k + 64], ident_b[:, :])
                    A("copy", out=KTx[:, :, :], in_=tpv[0:64, :, :])
                    ktp = KTx
                else:
                    ktp = KT[(n - 1) % 3]
                V("tensor_copy", out=qkb, in_=qkv[:, 0:512])
                G("tensor_copy", out=VE[cur][:, :, 0:64], in_=qkv[:, 512:768].rearrange("p (h e) -> p h e", h=4))
                psT = next_psT()
                tqv = psT[:, 0:1024].rearrange("p (c t) -> p c t", c=8)
                for k in range(8):
                    T("transpose", tqv[0:64, k, :], qkb[:, 64 * k:64 * k + 64], ident_b[:, :])
                V("tensor_copy", out=QT[:, :, :], in_=tqv[0:64, 0:4, :])
                A("copy", out=KT[cur][:, :, :], in_=tqv[0:64, 4:8, :])
                for hp in range(2):
                    psD = psDs[hp]
                    pt = sm2s[bp][:, 512 * hp:512 * hp + 512]
                    for hh in range(2):
                        h = 2 * hp + hh
                        o = 256 * hh
                        T("matmul", psD[:, o:o + 128], lhsT=ktp[:, h, :], rhs=QT[:, h, :], start=True, stop=False)
                        T("matmul", psD[:, o:o + 128], lhsT=ident_b[:, :], rhs=mask_b[:, 0:128], start=False, stop=True)
                        T("matmul", psD[:, o + 128:o + 256], lhsT=KT[cur][:, h, :], rhs=QT[:, h, :], start=True, stop=False)
                        T("matmul", psD[:, o + 128:o + 256], lhsT=ident_b[:, :], rhs=mask_b[:, 128:256], start=False, stop=True)
                    A("activation", out=pt, in_=psD[:, 0:512], func=AF.Exp, scale=0.125)

            def c_post(n):
                g, d, r, qb = cblocks[n]
                a0 = r + d * 128 * qb
                a1 = a0 + 127 * d + 1
                bp = n & 1
                cur = n % 3
                og = ogs[bp]
                vep = VExs[bp] if qb == 0 else VE[(n - 1) % 3]
                pov = psCs[bp][:, 0:260].rearrange("p (h e) -> p h e", h=4)
                for hp in range(2):
                    pt = sm2s[bp][:, 512 * hp:512 * hp + 512]
                    for hh in range(2):
                        h = 2 * hp + hh
                        o = 256 * hh
                        T("matmul", pov[:, h, :], lhsT=pt[:, o:o + 128], rhs=vep[:, h, :], start=True, stop=False)
                        T("matmul", pov[:, h, :], lhsT=pt[:, o + 128:o + 256], rhs=VE[cur][:, h, :], start=False, stop=True)
                V("tensor_copy", out=og[:, :], in_=psCs[bp][:, 0:260])
                em.dma(OG[g, a0:a1:d, :], og[:, :])

            def stage_C():
                c_pre(0)
                for n in range(len(cblocks)):
                    if n + 1 < len(cblocks):
                        c_pre(n + 1)
                    c_post(n)

            qs = FA[0:32, 4096:5632]
            qsb = sm1s[0][0:32, 0:768]
            ksb = sm2s[0][0:32, 0:768]
            em.dma(qs, P[SEQ:NTOK, 0:1536])
            V("tensor_copy", out=qsb, in_=qs[:, 0:768])
            V("tensor_copy", out=ksb, in_=qs[:, 768:1536])
            psT = next_psT()
            t12 = psT[:, 0:384].rearrange("p (c t) -> p c t", c=12)
            for k in range(12):
                T("transpose", t12[0:64, k, :], qsb[:, 64 * k:64 * k + 64], ident_b[0:32, 0:32])
            V("tensor_copy", out=QTS[:, :, :], in_=t12[0:64, :, :])
            psT = next_psT()
            t12 = psT[:, 0:384].rearrange("p (c t) -> p c t", c=12)
            for k in range(12):
                T("transpose", t12[0:64, k, :], ksb[:, 64 * k:64 * k + 64], ident_b[0:32, 0:32])
            V("tensor_copy", out=KTS[:, :, :], in_=t12[0:64, :, :])
            vsn = FA[0:8, 5632:6400]
            ct4s = [FA[:, 0:2048].rearrange("p (b c) -> p b c", b=4), FA[:, 2048:4096].rearrange("p (b c) -> p b c", b=4)]
            kt4s = [WBIG[0:64, 0:2048].rearrange("p (c t) -> p c t", c=16),
                    WBIG[0:64, 2048:4096].rearrange("p (c t) -> p c t", c=16)]
            ve4s = [WBIG[:, 4096:5136].rearrange("p (b h e) -> p b h e", b=4, h=4),
                    WBIG[:, 5136:6176].rearrange("p (b h e) -> p b h e", b=4, h=4)]
            ckb4s = [sm1s[0][:, 0:1024].rearrange("p (b c) -> p b c", b=4), sm1s[1][:, 0:1024].rearrange("p (b c) -> p b c", b=4)]
            for i in range(2):
                V("memset", ve4s[i], 1.0)
            cb = 0
            for s in range(4):
                acc = accs[s & 1]
                rd = rds[s & 1]
                oas = junks[s & 1][0:8, 0:256]
                em.dma(vsn, P[SEQ + 8 * s:SEQ + 8 * s + 8, 1536:2304])
                V("tensor_copy", out=VES[:, :, 0:64], in_=vsn.rearrange("p (h e) -> p h e", h=12))
                V("memset", acc[0:8], 0.0)
                for g, (W, d) in enumerate(GROUPS):
                    nblk = W // 128 if g < 2 else 8
                    for b0 in range(0, nblk, 4):
                        nb = min(4, nblk - b0)
                        bp = cb & 1
                        cb += 1
                        ct, ckb, kt, ve, pts = ct4s[bp], ckb4s[bp], kt4s[bp], ve4s[bp], sm2s[bp][:, 0:128]
                        if g < 2:
                            em.dma(ct[:, 0:nb, :], caches[g][l, s, 128 * b0:128 * (b0 + nb), :].rearrange("(b p) c -> p b c", p=128))
                        else:
                            for b in range(nb):
                                bb = b0 + b
                                em.dma(ct[:, b, :], caches[g][l, s, 256 * bb:256 * bb + 256, :].rearrange("(m j) c -> m j c", j=16)[:, 0:8, :])
                        V("tensor_copy", out=ckb[:, 0:nb, :], in_=ct[:, 0:nb, 0:256])
                        G("tensor_copy", out=ve[:, 0:nb, :, 0:64], in_=ct[:, 0:nb, 256:512].rearrange("p b (h e) -> p b h e", h=4))
                        for b in range(nb):
                            tkv = psTs[b // 2][:, 512 * (b % 2):512 * (b % 2) + 512].rearrange("p (c t) -> p c t", c=4)
                            for h in range(4):
                                T("transpose", tkv[0:64, h, :], ckb[:, b, 64 * h:64 * h + 64], ident_b[:, :])
                        for half in range((nb + 1) // 2):
                            nbh = min(2, nb - 2 * half)
                            src = psTs[half][:, 0:512 * nbh].rearrange("p (c t) -> p c t", c=4 * nbh)
                            if half == 0:
                                A("copy", out=kt[:, 0:4 * nbh, :], in_=src[0:64, :, :])
                            else:
                                V("tensor_copy", out=kt[:, 8:8 + 4 * nbh, :], in_=src[0:64, :, :])
                        psD = psDs[bp]
                        pssv = psD[:, 0:128].rearrange("p (c q) -> p c q", c=16)
                        posv = psD[:, 128:388].rearrange("p (h e) -> p h e", h=4)
                        for b in range(nb):
                            for h in range(4):
                                T("matmul", pssv[:, 4 * b + h, :], lhsT=kt[:, 4 * b + h, :],
                                  rhs=QTS[:, 4 * g + h, 8 * s:8 * s + 8], start=True, stop=True)
                        A("activation", out=pts[:, 0:32 * nb], in_=psD[:, 0:32 * nb], func=AF.Exp, scale=0.125)
                        mv = 2 * g + (1 if b0 > 0 else 0)
                        V("tensor_tensor", out=pts[:, 0:32 * nb], in0=pts[:, 0:32 * nb], in1=smask_b[:, mv, 0:32 * nb], op=ALU.mult)
                        for h in range(4):
                            for b in range(nb):
                                T("matmul", posv[0:8, h, :], lhsT=pts[:, 32 * b + 8 * h:32 * b + 8 * h + 8], rhs=ve[:, b, h, :],
                                  start=(b == 0), stop=(b == nb - 1))
                        V("tensor_tensor", out=acc[0:8], in0=acc[0:8], in1=posv[0:8], op=ALU.add)
                bp = cb & 1
                cb += 1
                psD = psDs[bp]
                ptn = sm2s[bp][:, 128:224]
                pssn = psD[:, 0:96].rearrange("p (c q) -> p c q", c=12)
                posv = psD[:, 128:388].rearrange("p (h e) -> p h e", h=4)
                for c12 in range(12):
                    T("matmul", pssn[0:8, c12, :], lhsT=KTS[:, c12, 8 * s:8 * s + 8],
                      rhs=QTS[:, c12, 8 * s:8 * s + 8], start=True, stop=True)
                A("activation", out=ptn[0:8], in_=psD[0:8, 0:96], func=AF.Exp, scale=0.125)
                V("tensor_tensor", out=ptn[0:8], in0=ptn[0:8], in1=snew_b[:, :], op=ALU.mult)
                for h in range(4):
                    for g in range(3):
                        T("matmul", posv[0:8, h, :], lhsT=ptn[0:8, 8 * (4 * g + h):8 * (4 * g + h) + 8],
                          rhs=VES[0:8, 4 * g + h, :], start=(g == 0), stop=(g == 2))
                V("tensor_tensor", out=acc[0:8], in0=acc[0:8], in1=posv[0:8], op=ALU.add)
                V("reciprocal", out=rd[0:8], in_=acc[0:8, :, 64:65])
                V("tensor_tensor", out=oas.rearrange("p (h e) -> p h e", h=4), in0=acc[0:8, :, 0:64],
                  in1=rd[0:8].to_broadcast([8, 4, 64]), op=ALU.mult)
                em.dma(OA[8 * s:8 * s + 8, :], oas)

            LIN = WBIG[:, 0:512].rearrange("p (c d) -> p c d", c=4)
            for c in range(4):
                load_weight(LIN[:, c, :], plin[l, c, :, :], 128, 128)
            em.dma(pscale[:], pscaleT[l, :, :])
            HS = SEQ // 2
            NJ = HS + 16
            sptx = junks[0][0:16, 0:512]
            em.dma(sptx, XGs[2][640:656, :])
            ue = FA[:, 0:NJ]
            sA = FA[:, NJ:2 * NJ]
            sB = FA[:, 2 * NJ:3 * NJ]
            pbts = [WBIG[:, 2048:2048 + HS], WBIG[:, 2048 + HS:2048 + 2 * HS]]
            obts = [WBIG[:, 2048 + 2 * HS:2048 + 3 * HS], WBIG[:, 2048 + 3 * HS:2048 + 4 * HS]]
            for c, w in enumerate(POOLW):
                for hh in range(2):
                    pbt, obt = pbts[hh], obts[hh]
                    t0 = HS * hh
                    if hh == 0:
                        T("transpose", psC[:, 64:80], sptx[0:16, 128 * c:128 * c + 128], ident_f[0:16, 0:16])
                        V("tensor_scalar", out=ue[:, 0:16], in0=psC[:, 64:80], scalar1=flag[:, 0:1], scalar2=None,
                          op0=ALU.mult)
                    else:
                        em.dma(ue[:, 0:16], UT[c, :, t0 - 16:t0])
                    em.dma(ue[:, 16:NJ], UT[c, :, t0:t0 + HS])
                    cur = ue
                    step = 1
                    bufs = [sA, sB]
                    bi = 0
                    while step < w:
                        lo = 2 * step - 1
                        nxt = bufs[bi]
                        bi ^= 1
                        V("tensor_tensor", out=nxt[:, lo:NJ], in0=cur[:, lo:NJ], in1=cur[:, lo - step:NJ - step], op=ALU.add)
                        cur = nxt
                        step *= 2
                    V("scalar_tensor_tensor", out=pbt, in0=cur[:, 16:NJ], scalar=1.0 / w,
                      in1=ue[:, 16:NJ], op0=ALU.mult, op1=ALU.subtract)
                    if hh == 0:
                        V("tensor_tensor", out=fix16[:, :], in0=cur[:, 16:32], in1=rcfix[:, c, :], op=ALU.mult)
                        V("tensor_tensor", out=pbt[:, 0:16], in0=fix16[:, :], in1=ue[:, 16:32], op=ALU.subtract)
                    for tt in range(HS // 512):
                        psA = next_psA()
                        T("matmul", psA[:, 0:512], lhsT=LIN[:, c, :], rhs=pbt[:, 512 * tt:512 * tt + 512],
                          start=True, stop=True)
                        V("tensor_scalar", out=obt[:, 512 * tt:512 * tt + 512], in0=psA[:, 0:512],
                          scalar1=pscale[:, c:c + 1], scalar2=None, op0=ALU.mult)
                    em.dma(OBT[c, :, t0:t0 + HS], obt)
                xs = xts[c & 1]
                ues = xs[:, 0:96].rearrange("p (s j) -> p s j", s=4)
                sAs = xs[:, 96:192].rearrange("p (s j) -> p s j", s=4)
                sBs = xs[:, 192:288].rearrange("p (s j) -> p s j", s=4)
                V("memset", ues[:, :, 0:1], 0.0)
                for s in range(4):
                    spt = sss_big[s & 1][0:15, 0:512]
                    em.dma(spt, spool[l, s, :, :])
                    T("transpose", psC[:, 16 * s:16 * s + 15], spt[0:15, 128 * c:128 * c + 128], ident_f[0:15, 0:15])
                    V("tensor_copy", out=ues[:, s, 1:16], in_=psC[:, 16 * s:16 * s + 15])
                em.dma(ues[:, :, 16:24], UT[c, :, SEQ:NTOK].rearrange("f (s t) -> f s t", s=4))
                cur = ues
                step = 1
                bufs = [sAs, sBs]
                bi = 0
                while step < w:
                    lo = 2 * step - 1
                    nxt = bufs[bi]
                    bi ^= 1
                    V("tensor_tensor", out=nxt[:, :, lo:24], in0=cur[:, :, lo:24], in1=cur[:, :, lo - step:24 - step], op=ALU.add)
                    cur = nxt
                    step *= 2
                sm1, sm2 = sm1s[c & 1], sm2s[c & 1]
                pbs = sm1[:, 0:32].rearrange("p (s t) -> p s t", s=4)
                V("scalar_tensor_tensor", out=pbs, in0=cur[:, :, 16:24], scalar=1.0 / w,
                  in1=ues[:, :, 16:24], op0=ALU.mult, op1=ALU.subtract)
                psA = next_psA()
                T("matmul", psA[:, 0:32], lhsT=LIN[:, c, :], rhs=sm1[:, 0:32], start=True, stop=True)
                V("tensor_scalar", out=sm2[:, 0:32], in0=psA[:, 0:32], scalar1=pscale[:, c:c + 1],
                  scalar2=None, op0=ALU.mult)
                em.dma(OBT[c, :, SEQ:NTOK], sm2[:, 0:32])

            stage_C()

            WPA = WBIG[0:64, 0:4096].rearrange("p (h c) -> p h c", h=4)
            WPB = WBIG[:, 4096:8192].rearrange("p (k c) -> p k c", k=4)
            WO = WBIG[:, 8192:16384].rearrange("p (k c) -> p k c", k=8)
            for h in range(4):
                load_weight(WPA[:, h, :], w_pa[l, 64 * h:64 * h + 64, :], 64, D)
            for k in range(4):
                load_weight(WPB[:, k, :], w_pb[l, 128 * k:128 * k + 128, :], 128, D)
            for k in range(8):
                load_weight(WO[:, k, :], w_o[l, 128 * k:128 * k + 128, :], 128, D)
            mTs = [WBIG[:, 16384:17408].rearrange("p (k t) -> p k t", k=8),
                   WBIG[:, 17408:18432].rearrange("p (k t) -> p k t", k=8)]
            def m_bufs(tb):
                pr = tb & 1
                rows = 128 if tb < NPB else NSMP
                return (rows, tb * 128, xts[pr], junks[pr], hTs[pr], accs[pr], acc2s[pr], acc3s[pr], rds[pr],
                        sm1s[pr][:, 0:256], sm2s[pr][:, 0:1024], oaTs[pr], mTs[pr],
                        projs[pr][:, 0:2048], projs[pr][:, 2048:3072], projs[pr][:, 3072:4096])

            def m_pre(tb):
                rows, r0, xt, junk, hT, acc, acc2, acc3, rd, oab, mxb, oaT, mT, gt, mx, tmpm = m_bufs(tb)
                if tb < NPB:
                    em.dma(acc[:].rearrange("p h e -> p (h e)"), OG[0, r0:r0 + 128, :])
                    em.dma(acc2[:].rearrange("p h e -> p (h e)"), OG[1, r0:r0 + 128, :])
                    em.dma(acc3[:].rearrange("p h e -> p (h e)"), OG[2, r0:r0 + 128, :])
                    G("tensor_tensor", out=acc[:], in0=acc[:], in1=acc2[:], op=ALU.add)
                    G("tensor_tensor", out=acc[:], in0=acc[:], in1=acc3[:], op=ALU.add)
                    V("reciprocal", out=rd[:], in_=acc[:, :, 64:65])
                    V("tensor_tensor", out=oab.rearrange("p (h e) -> p h e", h=4), in0=acc[:, :, 0:64],
                      in1=rd[:].to_broadcast([128, 4, 64]), op=ALU.mult)
                else:
                    em.dma(junk[0:32, 0:256], OA[:, :])
                    V("tensor_copy", out=oab[0:32], in_=junk[0:32, 0:256])
                em.dma(hT[:, 0:4, :rows], OBT.rearrange("c f t -> f c t")[:, :, r0:r0 + rows])
                em.dma(gt[:rows], P[r0:r0 + rows, 2816:INW])
                em.dma(xt[:rows], xsrc[r0:r0 + rows, :])
                transposes_bf(oab, rows, 4, 64, oaT)

            def m_mm1(tb):
                rows, r0, xt, junk, hT, acc, acc2, acc3, rd, oab, mxb, oaT, mT, gt, mx, tmpm = m_bufs(tb)
                for ng in range(2):
                    psA = next_psA()
                    for h in range(4):
                        T("matmul", psA[:rows, 0:512], lhsT=oaT[:, h, :rows], rhs=WPA[:, h, 512 * ng:512 * ng + 512],
                          start=(h == 0), stop=(h == 3))
                    V("tensor_tensor", out=mx[:rows, 512 * ng:512 * ng + 512], in0=psA[:rows, 0:512],
                      in1=gt[:rows, 512 * ng:512 * ng + 512], op=ALU.mult)
                    psA = next_psA()
                    for k in range(4):
                        T("matmul", psA[:rows, 0:512], lhsT=hT[:, k, :rows], rhs=WPB[:, k, 512 * ng:512 * ng + 512],
                          start=(k == 0), stop=(k == 3))
                    V("tensor_tensor", out=tmpm[:rows, 512 * ng:512 * ng + 512], in0=psA[:rows, 0:512],
                      in1=gt[:rows, 1024 + 512 * ng:1024 + 512 * ng + 512], op=ALU.mult)
                G("tensor_tensor", out=mxb[:rows], in0=mx[:rows], in1=tmpm[:rows], op=ALU.add)

            def m_T2(tb):
                rows, r0, xt, junk, hT, acc, acc2, acc3, rd, oab, mxb, oaT, mT, gt, mx, tmpm = m_bufs(tb)
                transposes_bf(mxb, rows, 8, 128, mT)

            def m_mm2(tb):
                rows, r0, xt, junk, hT, acc, acc2, acc3, rd, oab, mxb, oaT, mT, gt, mx, tmpm = m_bufs(tb)
                for ng in range(2):
                    psA = next_psA()
                    for k in range(8):
                        T("matmul", psA[:rows, 0:512], lhsT=mT[:, k, :rows], rhs=WO[:, k, 512 * ng:512 * ng + 512],
                          start=(k == 0), stop=(k == 7))
                    V("tensor_tensor", out=junk[:rows, 512 * ng:512 * ng + 512], in0=psA[:rows, 0:512],
                      in1=xt[:rows, 512 * ng:512 * ng + 512], op=ALU.add)
                em.dma(X[r0:r0 + rows, :], junk[:rows, :])

            m_pre(0)
            m_mm1(0)
            for tb in range(NBLK):
                if tb + 1 < NBLK:
                    m_pre(tb + 1)
                m_T2(tb)
                if tb + 1 < NBLK:
                    m_mm1(tb + 1)
                m_mm2(tb)

            em.dma(gbc[:], norm2[l:l + 1, :].partition_broadcast(128))
            WUP = WBIG[:, 0:16384].rearrange("p (k c) -> p k c", k=8)
            WDN = WBIG[:, 16384:32768].rearrange("p (k c) -> p k c", k=16)
            abs_ = [WBIG[:, 32768:34816], WBIG[:, 34816:36864]]
            aTs = [WBIG[:, 36864:38912].rearrange("p (k t) -> p k t", k=16),
                   WBIG[:, 38912:40960].rearrange("p (k t) -> p k t", k=16)]
            for hf in range(2):
                for k in range(8):
                    load_weight(WUP[:, k, :], w_up[l, 128 * k:128 * k + 128, 2048 * hf:2048 * hf + 2048], 128, 2048)
                for k in range(16):
                    load_weight(WDN[:, k, :], w_down[l, 2048 * hf + 128 * k:2048 * hf + 128 * k + 128, :], 128, D)
                def d_bufs(tb):
                    pr = tb & 1
                    rows = 128 if tb < NPB else NSMP
                    return (rows, tb * 128, xts[pr], junks[pr], sss[pr], hbs[pr], hTs[pr], abs_[pr], aTs[pr],
                            projs[0][:, 1024 * pr:1024 * pr + 1024], projs[1][:, 1024 * pr:1024 * pr + 1024])

                def d_pre(tb):
                    rows, r0, xt, junk, ss, hb, hT, ab, aT, x2t, outb = d_bufs(tb)
                    em.dma(xt[:rows], X[r0:r0 + rows, :])
                    if hf == 1:
                        em.dma(x2t[:rows], X2[r0:r0 + rows, :])
                    rmsnorm_rows(xt, junk, ss, rows)
                    V("scalar_tensor_tensor", out=hb[:rows], in0=xt[:rows], scalar=ss[:rows, 3:4],
                      in1=gbc[:rows], op0=ALU.mult, op1=ALU.mult)
                    transposes_bf(hb, rows, 8, 128, hT)

                def d_up(tb):
                    rows, r0, xt, junk, ss, hb, hT, ab, aT, x2t, outb = d_bufs(tb)
                    rls = [projs[0][:, 2048:2560], projs[0][:, 2560:3072]]
                    for ng in range(4):
                        psA = next_psA()
                        rl = rls[ng & 1]
                        for k in range(8):
                            T("matmul", psA[:rows, 0:512], lhsT=hT[:, k, :rows], rhs=WUP[:, k, 512 * ng:512 * ng + 512],
                              start=(k == 0), stop=(k == 7))
                        A("activation", out=rl[:rows], in_=psA[:rows, 0:512], func=AF.Relu)
                        G("tensor_tensor", out=ab[:rows, 512 * ng:512 * ng + 512], in0=rl[:rows], in1=rl[:rows], op=ALU.mult)

                def d_abT(tb):
                    rows, r0, xt, junk, ss, hb, hT, ab, aT, x2t, outb = d_bufs(tb)
                    for j in range(2):
                        psT = next_psT()
                        pv = psT[:, 0:1024].rearrange("p (c t) -> p c t", c=8)
                        for k in range(8):
                            T("transpose", pv[:, k, :rows], ab[:rows, (8 * j + k) * 128:(8 * j + k + 1) * 128],
                              ident_b[:rows, :rows])
                        if j == 0:
                            V("tensor_copy", out=aT[:, 8 * j:8 * j + 8, :rows], in_=pv[:, :, :rows])
                        else:
                            A("copy", out=aT[:, 8 * j:8 * j + 8, :rows], in_=pv[:, :, :rows])

                def d_down(tb):
                    rows, r0, xt, junk, ss, hb, hT, ab, aT, x2t, outb = d_bufs(tb)
                    res_src = xt if hf == 0 else x2t
                    for ng in range(2):
                        psA = next_psA()
                        for k in range(16):
                            T("matmul", psA[:rows, 0:512], lhsT=aT[:, k, :rows], rhs=WDN[:, k, 512 * ng:512 * ng + 512],
                              start=(k == 0), stop=(k == 15))
                        V("tensor_tensor", out=outb[:rows, 512 * ng:512 * ng + 512], in0=psA[:rows, 0:512],
                          in1=res_src[:rows, 512 * ng:512 * ng + 512], op=ALU.add)
                    em.dma((X2 if hf == 0 else X)[r0:r0 + rows, :], outb[:rows, :])

                d_pre(0)
                d_up(0)
                for tb in range(NBLK):
                    if tb + 1 < NBLK:
                        d_pre(tb + 1)
                    d_abT(tb)
                    if tb + 1 < NBLK:
                        d_up(tb + 1)
                    d_down(tb)

        em.dma(gbc[:], fnorm[0:1, :].partition_broadcast(128))
        for tb in range(NBLK):
            rows = 128 if tb < NPB else NSMP
            r0 = tb * 128
            pr = tb & 1
            xt, junk, ss = xts[pr], junks[pr], sss[pr]
            yo = proj[:, 1024 * pr:1024 * pr + 1024]
            em.dma(xt[:rows], X[r0:r0 + rows, :])
            rmsnorm_rows(xt, junk, ss, rows)
            V("scalar_tensor_tensor", out=yo[:rows], in0=xt[:rows], scalar=ss[:rows, 3:4],
              in1=gbc[:rows], op0=ALU.mult, op1=ALU.mult)
            em.dma(y[r0:r0 + rows, :], yo[:rows, :])
        em.flush()

        with nc.Block() as block:
            @block.sync
            def _(e):
                em.replay("sync", e)

            @block.tensor
            def _(e):
                em.replay("tensor", e)

            @block.vector
            def _(e):
                em.replay("vector", e)

            @block.scalar
            def _(e):
                em.replay("scalar", e)

            @block.gpsimd
            def _(e):
                em.replay("gpsimd", e)
    return nc


def _tables():
    pos = np.concatenate([np.arange(FULLSEQ), np.tile(8192 + np.arange(8), 4)]).astype(np.float32)
    inv = (500000.0 ** (-np.arange(0, 16, 2, dtype=np.float32) / 16)).astype(np.float32)
    ang = pos[:, None] * inv[None, :]
    cstab = np.concatenate([np.cos(ang), np.sin(ang)], axis=1).astype(np.float32)
    i = np.arange(128)[:, None]
    j = np.arange(128)[None, :]
    maskpc = np.where(np.concatenate([(i >= j), (i <= j)], axis=1), 0.0, -30000.0).astype(np.float32)
    smask = np.zeros((3, 2, 128, 32), np.float32)
    snew = np.zeros((3, 8, 32), np.float32)
    for g, (W, d) in enumerate(GROUPS):
        for v in range(2):
            for ii in range(128):
                for t in range(8):
                    ok = ((ii - t) % d == 0) and (v == 1 or ii >= t)
                    if g == 2:
                        ok = (ii % 8) == t
                    for h in range(4):
                        smask[g, v, ii, 8 * h + t] = 1.0 if ok else 0.0
        for tk in range(8):
            for t in range(8):
                ok = (tk <= t) and ((t - tk) % d == 0)
                for h in range(4):
                    snew[g, tk, 8 * h + t] = 1.0 if ok else 0.0
    rcfix = np.zeros((2, 4, 128, 16), np.float32)
    for c, w in enumerate(POOLW):
        for t in range(16):
            rcfix[0, c, :, t] = 1.0 / min(w, t + 1)
            rcfix[1, c, :, t] = 1.0 / w
    smask4 = np.zeros((3, 2, 128, 4, 32), np.float32)
    for g in range(3):
        for fb in range(2):
            for b in range(4):
                smask4[g, fb, :, b, :] = smask[g, 0 if (fb == 0 and b == 0) else 1]
    smask = smask4.reshape(3, 2, 128, 128)
    snew = np.ascontiguousarray(snew.transpose(1, 0, 2).reshape(8, 96))
    return cstab, maskpc, smask, snew, rcfix


def kernel(x_prompt, x_sample, cache_kv_w128, cache_kv_w512, cache_kv_w2048, state_pool,
           norm1, w_in, w_pa, w_pb, pool_lin, pool_scale, w_o, norm2, w_up, w_down, final_norm):
    f = lambda a: np.ascontiguousarray(np.asarray(a, dtype=np.float32))
    x_prompt, x_sample = f(x_prompt), f(x_sample)
    cch = [f(cache_kv_w128), f(cache_kv_w512), f(cache_kv_w2048)]
    state_pool = f(state_pool)
    cstab, maskpc, smask, snew, rcfix = _tables()
    shared = {
        "norm1": f(norm1), "norm2": f(norm2), "fnorm": f(final_norm).reshape(1, D),
        "w_in": f(w_in), "w_pa": f(w_pa), "w_pb": f(w_pb), "plin": f(pool_lin),
        "pscaleT": np.ascontiguousarray(f(pool_scale).reshape(DEPTH, 4, 128).transpose(0, 2, 1)),
        "w_o": f(w_o), "w_up": f(w_up), "w_down": f(w_down),
        "ident": np.eye(128, dtype=np.float32), "maskpc": maskpc,
        "smask": smask, "snew": snew,
    }
    in_maps = []
    for c in range(8):
        m = dict(shared)
        b, hf = c // 2, c % 2
        m["xin"] = np.ascontiguousarray(np.concatenate(
            [x_prompt[b, SEQ * hf:SEQ * hf + SEQ], x_sample[4 * c:4 * c + 4].reshape(NSMP, D)], axis=0))
        m["cstab"] = np.ascontiguousarray(np.concatenate([cstab[SEQ * hf:SEQ * hf + SEQ], cstab[FULLSEQ:]], axis=0))
        m["rcfix"] = np.ascontiguousarray(rcfix[hf])
        m["flag"] = np.full((128, 1), float(hf), np.float32)
        for g in range(3):
            W = GROUPS[g][0]
            m["cache%d" % g] = np.ascontiguousarray(cch[g][:, 4 * c:4 * c + 4].reshape(DEPTH, 4, W, 512))
        m["spool"] = np.ascontiguousarray(state_pool[:, 4 * c:4 * c + 4])
        in_maps.append(m)
    nc = build_program()
    res = run_bass_kernel_spmd(nc, in_maps, core_ids=list(range(8)))
    R = res.results
    y_prompt = np.stack([np.concatenate([R[2 * b]["y"][:SEQ], R[2 * b + 1]["y"][:SEQ]], axis=0) for b in range(4)], axis=0)
    y_sample = np.concatenate([R[c]["y"][SEQ:].reshape(4, 8, D) for c in range(8)], axis=0)
    outs = [y_prompt.astype(np.float32), y_sample.astype(np.float32)]
    for g in range(3):
        W = GROUPS[g][0]
        outs.append(np.stack([R[2 * b + 1]["kvp%d" % g].reshape(DEPTH, W, 2, 4, 64) for b in range(4)], axis=1))
    outs.append(np.stack([R[2 * b + 1]["poolp"] for b in range(4)], axis=1))
    for g in range(3):
        W = GROUPS[g][0]
        outs.append(np.concatenate([R[c]["kvs%d" % g].reshape(DEPTH, 4, W, 2, 4, 64) for c in range(8)], axis=1))
    outs.append(np.concatenate([R[c]["pools"] for c in range(8)], axis=1))
    return tuple(np.ascontiguousarray(o, dtype=np.float32) for o in outs)
```
